# Optimizing a Trainium2 kernel written in Bass

```python
import math
import jax, jax.numpy as jnp
from jax import lax
import numpy as np

D_MODEL = 1024
BATCH = 8
SEQ = 4096
DEPTH = 1

PLE_DIM = 256
HEAD_DIM = 64
A_Q_HEADS = 8
A_KV_HEADS = 2
A_GROUP = A_Q_HEADS // A_KV_HEADS
A_WINDOW = 128
B_HEADS = 8
B_PATTERNS = ((128, 1), (512, 4), (2048, 16))
N_HEADS_TOTAL = A_Q_HEADS + B_HEADS
A_Q = A_Q_HEADS * HEAD_DIM
A_KV = A_KV_HEADS * HEAD_DIM
B_W = B_HEADS * HEAD_DIM
D_IN = A_Q + 2 * A_KV + 3 * B_W
D_MIX = A_Q + B_W
D_FF = 2816
NUM_BUCKETS = 32
MAX_DISTANCE = 2048
BLOCK = 128
EPS = 1e-6
NEG_INF = -1e30

kernel_name = "hymba_swa_sink_dilated_macaron_layer"


def rms_norm(x, g):
    xf = x.astype(jnp.float32)
    y = xf * lax.rsqrt(jnp.mean(xf * xf, axis=-1, keepdims=True) + EPS)
    return (y * g.astype(jnp.float32)).astype(x.dtype)


def swiglu(x, w_gu, w_down):
    g, u = jnp.split(x @ w_gu, 2, axis=-1)
    return (jax.nn.silu(g) * u) @ w_down


def t5_bucket(dist):
    max_exact = NUM_BUCKETS // 2
    n = jnp.maximum(dist, 0)
    nf = jnp.maximum(n, 1).astype(jnp.float32)
    large = max_exact + (jnp.log(nf / max_exact) / math.log(MAX_DISTANCE / max_exact)
                         * (NUM_BUCKETS - max_exact)).astype(jnp.int32)
    large = jnp.minimum(large, NUM_BUCKETS - 1)
    return jnp.where(n < max_exact, n, large)


def banded_attention(q, k, v, rel_bias, max_dist, stride):
    n, L, hkv, grp, dh = q.shape
    bq = math.gcd(L, BLOCK)
    nb = L // bq
    nk = bq + max_dist
    pad = ((0, 0), (max_dist, 0), (0, 0), (0, 0))
    k_pad = jnp.pad(k, pad)
    v_pad = jnp.pad(v, pad)
    key_idx = jnp.arange(nb)[:, None] * bq + jnp.arange(nk)[None, :]
    kb = k_pad[:, key_idx]
    vb = v_pad[:, key_idx]
    qb = q.reshape(n, nb, bq, hkv, grp, dh)
    logits = jnp.einsum('nbqhgd,nbkhd->nbhgqk', qb, kb,
                        preferred_element_type=jnp.float32) * (dh ** -0.5)
    rel = jnp.arange(bq)[:, None] + max_dist - jnp.arange(nk)[None, :]
    bias = rel_bias[t5_bucket(rel * stride)].astype(jnp.float32)
    bias = bias.reshape(bq, nk, hkv, grp).transpose(2, 3, 0, 1)
    in_band = (rel >= 0) & (rel <= max_dist)
    key_pos = key_idx - max_dist
    valid = in_band[None] & (key_pos >= 0)[:, None, :]
    logits = jnp.where(valid[None, :, None, None], logits + bias, NEG_INF)
    m = jnp.max(logits, axis=-1)
    pr = jnp.exp(logits - m[..., None])
    s = jnp.sum(pr, axis=-1)
    o = jnp.einsum('nbhgqk,nbkhd->nbqhgd', pr, vb.astype(jnp.float32))
    o = o.reshape(n, L, hkv, grp, dh)
    m = m.transpose(0, 1, 4, 2, 3).reshape(n, L, hkv, grp)
    s = s.transpose(0, 1, 4, 2, 3).reshape(n, L, hkv, grp)
    return o, m, s


def to_classes(t, r):
    b, s = t.shape[:2]
    rest = t.shape[2:]
    t = jnp.moveaxis(t.reshape((b, s // r, r) + rest), 2, 1)
    return t.reshape((b * r, s // r) + rest)


def from_classes(t, b, r):
    n, L = t.shape[:2]
    rest = t.shape[2:]
    t = jnp.moveaxis(t.reshape((b, r, L) + rest), 1, 2)
    return t.reshape((b, L * r) + rest)


def sink_swa_gqa(q_a, k_a, v_a, sinks, rel_bias_a):
    b, s, _ = q_a.shape
    q = q_a.reshape(b, s, A_KV_HEADS, A_GROUP, HEAD_DIM)
    k = k_a.reshape(b, s, A_KV_HEADS, HEAD_DIM)
    v = v_a.reshape(b, s, A_KV_HEADS, HEAD_DIM)
    o, m, den = banded_attention(q, k, v, rel_bias_a, A_WINDOW - 1, 1)
    sink = sinks.reshape(A_KV_HEADS, A_GROUP).astype(jnp.float32)
    m_all = jnp.maximum(m, sink)
    scale = jnp.exp(m - m_all)
    total = den * scale + jnp.exp(sink - m_all)
    o = o * (scale / total)[..., None]
    return o.reshape(b, s, A_Q)


def dilated_mixture(q_b, k_b, v_b, rel_bias_b):
    b, s, _ = q_b.shape
    q = q_b.reshape(b, s, B_HEADS, 1, HEAD_DIM)
    k = k_b.reshape(b, s, B_HEADS, HEAD_DIM)
    v = v_b.reshape(b, s, B_HEADS, HEAD_DIM)
    outs, maxes, dens = [], [], []
    for window, dil in B_PATTERNS:
        o, m, den = banded_attention(to_classes(q, dil), to_classes(k, dil), to_classes(v, dil),
                                     rel_bias_b, window // dil, dil)
        outs.append(from_classes(o, b, dil))
        maxes.append(from_classes(m, b, dil))
        dens.append(from_classes(den, b, dil))
    m_stack = jnp.stack(maxes)
    m_all = jnp.max(m_stack, axis=0)
    w = jnp.exp(m_stack - m_all)
    num = jnp.sum(w[..., None] * jnp.stack(outs), axis=0)
    den_all = jnp.sum(w * jnp.stack(dens), axis=0)
    return (num / den_all[..., None]).reshape(b, s, B_W)


def setup_inputs(seed: int = 0) -> dict:
    key = jax.random.key(seed)
    ks = jax.random.split(key, 24)
    f32 = jnp.float32

    def w(k, shape, fan_in):
        return jax.random.normal(k, shape, f32) * (fan_in ** -0.5)

    def gain(k, shape):
        return 1.0 + 0.05 * jax.random.normal(k, shape, f32)

    def small(k, shape, scale=0.02):
        return scale * jax.random.normal(k, shape, f32)

    D = D_MODEL
    return {
        "x": jax.random.normal(ks[0], (BATCH, SEQ, D), f32),
        "p": jax.random.normal(ks[1], (DEPTH, BATCH, SEQ, PLE_DIM), f32),
        "rel_bias": small(ks[2], (NUM_BUCKETS, N_HEADS_TOTAL), 0.5),
        "ffn1_pre_g": gain(ks[3], (DEPTH, D)),
        "ffn1_w_gu": w(ks[4], (DEPTH, D, 2 * D_FF), D),
        "ffn1_w_down": w(ks[5], (DEPTH, D_FF, D), D_FF),
        "ffn1_post_g": gain(ks[6], (DEPTH, D)),
        "attn_pre_g": gain(ks[7], (DEPTH, D)),
        "w_in": w(ks[8], (DEPTH, D, D_IN), D),
        "b_in": small(ks[9], (DEPTH, D_IN)),
        "sinks": small(ks[10], (DEPTH, A_Q_HEADS), 0.5),
        "w_out": w(ks[11], (DEPTH, D_MIX, D), D_MIX),
        "b_out": small(ks[12], (DEPTH, D)),
        "attn_post_g": gain(ks[13], (DEPTH, D)),
        "ffn2_pre_g": gain(ks[14], (DEPTH, D)),
        "ffn2_w_gu": w(ks[15], (DEPTH, D, 2 * D_FF), D),
        "ffn2_w_down": w(ks[16], (DEPTH, D_FF, D), D_FF),
        "ffn2_post_g": gain(ks[17], (DEPTH, D)),
        "ple_pre_g": gain(ks[18], (DEPTH, D)),
        "w_ple_gate": w(ks[19], (DEPTH, D, D), D),
        "w_ple_proj": w(ks[20], (DEPTH, PLE_DIM, D), PLE_DIM),
        "ple_post_g": gain(ks[21], (DEPTH, D)),
    }


def reference(x, p, rel_bias, ffn1_pre_g, ffn1_w_gu, ffn1_w_down, ffn1_post_g,
              attn_pre_g, w_in, b_in, sinks, w_out, b_out, attn_post_g,
              ffn2_pre_g, ffn2_w_gu, ffn2_w_down, ffn2_post_g,
              ple_pre_g, w_ple_gate, w_ple_proj, ple_post_g):
    h = x
    splits = [A_Q, A_Q + A_KV, A_Q + 2 * A_KV, A_Q + 2 * A_KV + B_W, A_Q + 2 * A_KV + 2 * B_W]
    for i in range(DEPTH):
        f = swiglu(rms_norm(h, ffn1_pre_g[i]), ffn1_w_gu[i], ffn1_w_down[i])
        h = h + 0.5 * rms_norm(f, ffn1_post_g[i])

        z = rms_norm(h, attn_pre_g[i]) @ w_in[i] + b_in[i]
        q_a, k_a, v_a, q_b, k_b, v_b = jnp.split(z, splits, axis=-1)
        out_a = sink_swa_gqa(q_a, k_a, v_a, sinks[i], rel_bias[:, :A_Q_HEADS])
        out_b = dilated_mixture(q_b, k_b, v_b, rel_bias[:, A_Q_HEADS:])
        mix = jnp.concatenate([out_a, out_b], axis=-1).astype(x.dtype)
        att = mix @ w_out[i] + b_out[i]
        h = h + rms_norm(att, attn_post_g[i])

        f = swiglu(rms_norm(h, ffn2_pre_g[i]), ffn2_w_gu[i], ffn2_w_down[i])
        h = h + 0.5 * rms_norm(f, ffn2_post_g[i])

        gate = jax.nn.sigmoid(rms_norm(h, ple_pre_g[i]) @ w_ple_gate[i])
        e = p[i] @ w_ple_proj[i]
        h = h + rms_norm(gate * e, ple_post_g[i])
    return h
```

```python
import numpy as np
import concourse.bass as bass
import concourse.mybir as mybir
from concourse.bass_utils import run_bass_kernel_spmd

F32 = mybir.dt.float32
BF16 = mybir.dt.bfloat16
AF = mybir.ActivationFunctionType
ALU = mybir.AluOpType

D = 1024
SEQ = 4096
NB = 8
T = 512
NCH = SEQ // T
DFF = 2816
NHT = DFF // 128
DIN = 2304
PLE = 256
EPS = 1e-6
NEG = -30000.0
NSLOT = 4
SLOT_ELEMS = 2816

E_A, E_1, E_4, E_16 = 0, 2048, 4096, 6144
NV_G = 0
NV_BIN = 64
NV_BOUT = 82
NV_SINK = 90
NV = 94
G_FFN1_PRE, G_FFN1_POST, G_ATT_PRE, G_ATT_POST, G_FFN2_PRE, G_FFN2_POST, G_PLE_PRE, G_PLE_POST = range(8)

DEBUG_STAGE = None


class _Op:
    __slots__ = ("eng", "fn", "deps", "dma", "signal", "val")


class Sched:
    ENGS = ("pe", "act", "dve", "pool", "sp")
    STRICT = True

    def __init__(self):
        self.ops = {e: [] for e in self.ENGS}
        self.last_w = {}
        self.readers = {}
        self.dma_keys = []
        self.extra = {}

    def alias_barrier(self, from_keys, to_keys):
        deps = {}
        for k in from_keys:
            w = self.last_w.get(k)
            if w is not None:
                deps[id(w)] = w
            for r in self.readers.get(k, ()):
                deps[id(r)] = r
        for k in to_keys:
            self.extra.setdefault(k, {}).update(deps)

    def op(self, eng, fn, reads=(), writes=(), dma=None):
        o = _Op()
        o.eng, o.fn, o.dma, o.signal, o.val = eng, fn, dma, False, 0
        deps = {}
        for k in list(reads) + list(writes):
            ex = self.extra.pop(k, None)
            if ex:
                deps.update(ex)
        same = {}
        for k in reads:
            w = self.last_w.get(k)
            if w is not None:
                deps[id(w)] = w
                same[id(w)] = True
        for k in writes:
            w = self.last_w.get(k)
            if w is not None:
                deps[id(w)] = w
                same[id(w)] = True
            for r in self.readers.get(k, ()):
                deps[id(r)] = r
        o.deps = [d for d in deps.values()
                  if d is not o and (d.dma is not None or d.eng != eng or (self.STRICT and eng != "pe"))]
        for k in reads:
            self.readers.setdefault(k, []).append(o)
        for k in writes:
            self.last_w[k] = o
            self.readers[k] = []
        if dma is not None and dma not in self.dma_keys:
            self.dma_keys.append(dma)
        self.ops[eng].append(o)
        return o

    def finalize(self):
        for e in self.ENGS:
            for o in self.ops[e]:
                for d in o.deps:
                    d.signal = True
        dcount = {}
        for e in self.ENGS:
            c = 0
            for o in self.ops[e]:
                if o.dma is not None:
                    pass
                elif o.signal:
                    c += 1
                    o.val = c
        for e in self.ENGS:
            for o in self.ops[e]:
                if o.dma is not None:
                    dcount[o.dma] = dcount.get(o.dma, 0) + 16
                    o.val = dcount[o.dma]

    def emit(self, nc, block, sems):
        def make(eng_name):
            def body(E):
                waited = {}
                for o in self.ops[eng_name]:
                    need = {}
                    for d in o.deps:
                        sid = d.dma if d.dma is not None else d.eng
                        if d.val > need.get(sid, 0):
                            need[sid] = d.val
                    for sid, v in need.items():
                        if waited.get(sid, 0) >= v:
                            continue
                        E.wait_ge(sems[sid], v)
                        waited[sid] = v
                    ins = o.fn(E)
                    if ins is None:
                        continue
                    if o.dma is not None:
                        ins.then_inc(sems[o.dma], 16)
                    elif o.signal:
                        ins.then_inc(sems[o.eng], 1)
            return body
        block.tensor(make("pe"))
        block.scalar(make("act"))
        block.vector(make("dve"))
        block.gpsimd(make("pool"))
        block.sync(make("sp"))


def _t5_bucket(dist):
    n = np.maximum(dist, 0)
    nf = np.maximum(n, 1).astype(np.float64)
    large = 16 + (np.log(nf / 16.0) / np.log(2048.0 / 16.0) * 16.0).astype(np.int64)
    large = np.minimum(large, 31)
    return np.where(n < 16, n, large)


def _build_bias(rel_bias):
    rb = np.asarray(rel_bias, dtype=np.float32)
    E = np.full((128, 8192), NEG, dtype=np.float32)
    kk = np.arange(128)[:, None]
    i = np.arange(128)[None, :]
    u1 = i
    u4 = i
    for blk in range(2):
        off = 128 if blk == 0 else 0
        d1 = u1 - kk + off
        d4 = u4 - kk + off
        validA = (d1 >= 0) & (d1 <= 127)
        bA = _t5_bucket(d1)
        for g in range(2):
            for hh in range(4):
                h = hh + 4 * g
                col = E_A + (g * 2 + blk) * 512 + hh * 128
                E[:, col:col + 128] = np.where(validA, rb[bA, h], NEG)
        valid1 = (d1 >= 0) & (d1 <= 128)
        b1 = _t5_bucket(d1)
        valid4 = (d4 >= 0) & (d4 <= 128)
        b4 = _t5_bucket(d4 * 4)
        for pr in range(4):
            for eo in range(2):
                h = 8 + 2 * pr + eo
                col = E_1 + pr * 512 + blk * 256 + eo * 128
                E[:, col:col + 128] = np.where(valid1, rb[b1, h], NEG)
                col = E_4 + pr * 512 + blk * 256 + eo * 128
                E[:, col:col + 128] = np.where(valid4, rb[b4, h], NEG)
    i32 = np.arange(32)[None, :]
    for cm in range(4):
        for blk in range(2):
            off = 128 if blk == 0 else 0
            d16 = 32 * cm + i32 - kk + off
            valid = (d16 >= 0) & (d16 <= 128)
            b16 = _t5_bucket(d16 * 16)
            for pr in range(4):
                for eo in range(2):
                    h = 8 + 2 * pr + eo
                    col = E_16 + pr * 512 + (cm * 2 + blk) * 64 + eo * 32
                    E[:, col:col + 32] = np.where(valid, rb[b16, h], NEG)
    return E


_QA_HEAD_ORDER = [0, 4, 1, 5, 2, 6, 3, 7]


def _prep_shared(inp):
    f = lambda a: np.ascontiguousarray(np.asarray(a, dtype=np.float32))
    w_in = f(inp["w_in"][0])
    b_in = f(inp["b_in"][0])
    qcols = np.concatenate([np.arange(h * 64, h * 64 + 64) for h in _QA_HEAD_ORDER])
    perm = np.concatenate([qcols, np.arange(512, DIN)])
    w_in_p = np.ascontiguousarray(w_in[:, perm])
    b_in_p = b_in[perm]
    w_out = f(inp["w_out"][0])
    rperm = np.concatenate([qcols, np.arange(512, 1024)])
    w_out_p = np.ascontiguousarray(w_out[rperm, :])

    def gu_interleave(w):
        w = f(w)
        g = w[:, :DFF].reshape(D, NHT, 128)
        u = w[:, DFF:].reshape(D, NHT, 128)
        return np.ascontiguousarray(np.stack([g, u], axis=2).reshape(D, 2 * DFF))

    vecs = np.zeros((128, NV), dtype=np.float32)
    gains = [inp["ffn1_pre_g"], inp["ffn1_post_g"], inp["attn_pre_g"], inp["attn_post_g"],
             inp["ffn2_pre_g"], inp["ffn2_post_g"], inp["ple_pre_g"], inp["ple_post_g"]]
    for gi, g in enumerate(gains):
        vecs[:, NV_G + 8 * gi: NV_G + 8 * gi + 8] = f(g[0]).reshape(8, 128).T
    vecs[:, NV_BIN:NV_BIN + 18] = b_in_p.reshape(18, 128).T
    vecs[:, NV_BOUT:NV_BOUT + 8] = f(inp["b_out"][0]).reshape(8, 128).T
    sinks = f(inp["sinks"][0])
    for hh in range(4):
        vecs[0:64, NV_SINK + hh] = sinks[hh]
        vecs[64:128, NV_SINK + hh] = sinks[hh + 4]
    shared = {
        "w_gu1": gu_interleave(inp["ffn1_w_gu"][0]),
        "w_d1": f(inp["ffn1_w_down"][0]),
        "w_in": w_in_p,
        "w_out": w_out_p,
        "w_gu2": gu_interleave(inp["ffn2_w_gu"][0]),
        "w_d2": f(inp["ffn2_w_down"][0]),
        "w_gate": f(inp["w_ple_gate"][0]),
        "w_proj": f(inp["w_ple_proj"][0]),
        "vecs": vecs,
        "ebias": _build_bias(inp["rel_bias"]),
        "identf": np.eye(128, dtype=np.float32),
    }
    return shared


def build_program(nchunks=NCH, debug_stage=None):
    nc = bass.Bass("TRN2", target_bir_lowering=False)
    dt = lambda name, shape, kind: nc.dram_tensor(name, shape, F32, kind=kind).ap()
    x_d = dt("x", [SEQ, D], "ExternalInput")
    p_d = dt("p", [SEQ, PLE], "ExternalInput")
    wgu_d = [dt("w_gu1", [D, 2 * DFF], "ExternalInput"), dt("w_gu2", [D, 2 * DFF], "ExternalInput")]
    wd_d = [dt("w_d1", [DFF, D], "ExternalInput"), dt("w_d2", [DFF, D], "ExternalInput")]
    win_d = dt("w_in", [D, DIN], "ExternalInput")
    wout_d = dt("w_out", [D, D], "ExternalInput")
    wgate_d = dt("w_gate", [D, D], "ExternalInput")
    wproj_d = dt("w_proj", [PLE, D], "ExternalInput")
    vecs_d = dt("vecs", [128, NV], "ExternalInput")
    ebias_d = dt("ebias", [128, 8192], "ExternalInput")
    identf_d = dt("identf", [128, 128], "ExternalInput")
    out_d = dt("out", [SEQ, D], "ExternalOutput")

    S = Sched()
    from contextlib import ExitStack
    es = ExitStack()
    sb = lambda name, shape, dtype: es.enter_context(nc.sbuf_tensor(name, shape, dtype))
    ps = lambda name, shape, dtype: es.enter_context(nc.psum_tensor(name, shape, dtype))

    KaT = sb("KaT", [128, 1024], BF16)
    VaT = sb("VaT", [128, 1024], BF16)
    KbT = sb("KbT", [128, 4, SEQ], BF16)
    VbT = sb("VbT", [128, 4, SEQ], BF16)
    hT = sb("hT", [128, 8, T], F32)
    Ebf = sb("Ebf", [128, 8192], BF16)
    identf = sb("identf_sb", [128, 128], F32)
    identb = sb("identb", [128, 128], BF16)
    onesb = sb("onesb", [128, 128], BF16)
    vecs = sb("vecs_sb", [128, NV], F32)
    gsc = sb("gsc", [128, 64], F32)
    esink = sb("esink", [128, 4], F32)
    neghalf = sb("neghalf", [128, 4], F32)
    onesf = sb("onesf", [128, 128], F32)
    tsum_t = sb("tsum_t", [128, 4], F32)
    rstd_t = sb("rstd_t", [128, 4], F32)
    diag = sb("diag", [128, T], F32)
    fT = sb("fT", [128, 8, T], F32)
    xnT = sb("xnT", [128, 8, T], BF16)
    hid = sb("hid", [128, NHT, T], BF16)
    wring = sb("wring", [128, NSLOT, SLOT_ELEMS], BF16)
    QaP = sb("QaP", [128, 2, 4, T], BF16)
    QbD = sb("QbD", [128, 4, 2, T], BF16)
    tmpA = sb("tmpA", [128, T], F32)
    xin = sb("xin", [128, D], F32)
    xnb = sb("xnb", [128, D], BF16)
    ssx = sb("ssx", [128, 4], F32)
    tsx = sb("tsx", [128, 4], F32)
    rsx = sb("rsx", [128, 4], F32)
    sq = sb("sq", [128, 2, T], BF16)
    rden = sb("rden", [128, T], F32)
    pT = sb("pT", [128, 2, T], BF16)
    pin = sb("pin", [128, 4, PLE], BF16)
    fT_flat = fT[:, :, :].rearrange("p k t -> p (k t)")
    xs = fT_flat.rearrange("p (s f) -> p s f", s=4)
    Vt = fT_flat.bitcast(BF16).rearrange("p (n f) -> p n f", f=128)
    NPB = 4
    PB = [ps(f"pb{i}", [128, 512], F32) for i in range(NPB)]
    PBH = [PB[i][:, :].bitcast(BF16) for i in range(NPB)]
    NUM = ps("num", [128, 512], F32)
    NUM2 = ps("num2", [128, 512], F32)
    DEN = ps("den", [128, 512], F32)
    SSQ = ps("ssq", [128, 512], F32)
    NUMS = [(NUM, "NUM"), (NUM2, "NUM2")]
    DENS = [(DEN, "DEN"), (SSQ, "SSQ")]

    state = {"pb": 0, "sq": 0, "praw": 0}
    dbg_keys = []

    def next_pb():
        i = state["pb"]
        state["pb"] = (i + 1) % NPB
        return i

    blocks = []
    def chunk_blocks():
        bl = []
        if debug_stage == "x":
            return bl
        bl += [("gu", 0, j) for j in range(NHT)]
        bl += [("dn", 0, mp, kh) for mp in range(4) for kh in range(2)]
        if debug_stage == "ffn1":
            return bl
        bl += [("in", cb) for cb in range(9)]
        bl += [("out", cb) for cb in range(4)]
        if debug_stage in ("attn", "mix"):
            return bl
        bl += [("gu", 1, j) for j in range(NHT)]
        bl += [("dn", 1, mp, kh) for mp in range(4) for kh in range(2)]
        if debug_stage == "ffn2":
            return bl
        bl += [("gate", cb) for cb in range(4)]
        bl += [("proj",)]
        return bl
    for c in range(nchunks):
        blocks += chunk_blocks()
    wstate = {"next_issue": 0, "next_use": 0}

    def slot_view(s, k, c):
        return wring[:, s, 0:k * c].rearrange("p (k c) -> p k c", k=k)

    def issue_block():
        bi = wstate["next_issue"]
        if bi >= len(blocks):
            return
        wstate["next_issue"] = bi + 1
        b = blocks[bi]
        s = bi % NSLOT
        kind = b[0]
        if kind == "gu":
            src = wgu_d[b[1]][:, 256 * b[2]:256 * b[2] + 256].rearrange("(k p) c -> p k c", p=128)
            dst = slot_view(s, 8, 256)
        elif kind == "dn":
            mp, kh = b[2], b[3]
            src = wd_d[b[1]][kh * 1408:(kh + 1) * 1408, 256 * mp:256 * mp + 256].rearrange("(k p) c -> p k c", p=128)
            dst = slot_view(s, 11, 256)
        elif kind == "in":
            src = win_d[:, 256 * b[1]:256 * b[1] + 256].rearrange("(k p) c -> p k c", p=128)
            dst = slot_view(s, 8, 256)
        elif kind == "out":
            src = wout_d[:, 256 * b[1]:256 * b[1] + 256].rearrange("(k p) c -> p k c", p=128)
            dst = slot_view(s, 8, 256)
        elif kind == "gate":
            src = wgate_d[:, 256 * b[1]:256 * b[1] + 256].rearrange("(k p) c -> p k c", p=128)
            dst = slot_view(s, 8, 256)
        else:
            src = wproj_d[:, :].rearrange("(k p) c -> p k c", p=128)
            dst = slot_view(s, 2, 1024)
        S.op("pool", lambda E, dst=dst, src=src: E.dma_start(out=dst, in_=src),
             writes=[("w", s)], dma=("w", s))

    def use_block(expect):
        bi = wstate["next_use"]
        wstate["next_use"] = bi + 1
        assert blocks[bi][0] == expect, (blocks[bi], expect)
        return bi % NSLOT

    ALLFT = [("fT", i) for i in range(8)]
    S.op("sp", lambda E: E.dma_start(out=vecs[:, :], in_=vecs_d[:, :]), writes=["vecs"], dma="c0")
    S.op("sp", lambda E: E.dma_start(out=identf[:, :], in_=identf_d[:, :]), writes=["identf"], dma="c1")
    S.op("sp", lambda E: E.dma_start(out=fT_flat, in_=ebias_d[:, 0:4096]), writes=ALLFT, dma="c2")
    for _ in range(NSLOT):
        issue_block()
    S.op("dve", lambda E: E.memset(KaT[:, :], 0.0), writes=["KaT"])
    S.op("dve", lambda E: E.memset(VaT[:, :], 0.0), writes=["VaT"])
    S.op("dve", lambda E: E.memset(QaP[:, :, :, :].rearrange("p a b t -> p (a b t)"), 0.0), writes=["QaP"])
    S.op("dve", lambda E: E.memset(QbD[:, :, :, :].rearrange("p a b t -> p (a b t)"), 0.0), writes=["QbD"])
    for pr in range(4):
        S.op("dve", lambda E, pr=pr: E.memset(KbT[:, pr, :], 0.0), writes=["KbT"])
        S.op("dve", lambda E, pr=pr: E.memset(VbT[:, pr, :], 0.0), writes=["VbT"])
    S.op("dve", lambda E: E.memset(onesb[:, :], 1.0), writes=["onesb"])
    S.op("dve", lambda E: E.memset(onesf[:, :], 1.0), writes=["onesf"])
    S.op("dve", lambda E: E.memset(neghalf[:, :], -0.5), writes=["neghalf"])
    S.op("dve", lambda E: E.tensor_copy(identb[:, :], identf[:, :]), reads=["identf"], writes=["identb"])
    for gi in range(8):
        scl = 16.0 if gi in (G_FFN1_POST, G_FFN2_POST) else 32.0
        S.op("dve", lambda E, gi=gi, scl=scl: E.tensor_scalar(gsc[:, 8 * gi:8 * gi + 8], vecs[:, NV_G + 8 * gi:NV_G + 8 * gi + 8],
                                                              scl, None, ALU.mult), reads=["vecs"], writes=["gsc"])
    S.op("act", lambda E: E.activation(esink[:, :], vecs[:, NV_SINK:NV_SINK + 4], AF.Exp), reads=["vecs"], writes=["esink"])
    S.op("act", lambda E: E.activation(Ebf[:, 0:4096], fT_flat, AF.Identity, scale=8.0), reads=ALLFT, writes=["Ebf"])
    S.op("sp", lambda E: E.dma_start(out=fT_flat, in_=ebias_d[:, 4096:8192]), writes=ALLFT, dma="c3")
    S.op("act", lambda E: E.activation(Ebf[:, 4096:8192], fT_flat, AF.Identity, scale=8.0), reads=ALLFT, writes=["Ebf"])

    CONSTS = ["identf", "identb", "onesb", "neghalf", "gsc", "esink", "Ebf", "vecs"]

    def norm_stats_from(src_fn, src_keys, bias_fn=None):
        for k in range(8):
            si = state["sq"]
            state["sq"] ^= 1
            if bias_fn is None:
                S.op("act", lambda E, k=k, si=si: E.activation(sq[:, si, :], src_fn(k), AF.Square),
                     reads=src_keys(k), writes=[("sq", si)])
            else:
                S.op("act", lambda E, k=k, si=si: E.activation(sq[:, si, :], src_fn(k), AF.Square, bias=bias_fn(k)),
                     reads=src_keys(k) + ["vecs"], writes=[("sq", si)])
            ssq_accum(k, si)
        finish_rstd(D * EPS)

    def finish_rstd(epsv):
        S.op("dve", lambda E: E.tensor_scalar(tsum_t[:, :], DEN[:, 0:4], float(epsv), None, ALU.add), reads=["DEN"], writes=["tsum_t"])
        S.op("pool", lambda E: E.tensor_tensor(rstd_t[:, :], tsum_t[:, :], neghalf[:, :], ALU.pow),
             reads=["tsum_t", "neghalf"], writes=["rstd_t"])
        S.op("dve", lambda E: E.tensor_tensor(diag[:, :].rearrange("p (t j) -> p t j", t=4),
                                              identf[:, :].unsqueeze(1).broadcast_to([128, 4, 128]),
                                              rstd_t[:, :].unsqueeze(2).broadcast_to([128, 4, 128]), ALU.mult),
             reads=["rstd_t", "identf"], writes=[("diag", tt) for tt in range(4)])
        for tt in range(4):
            S.op("pe", lambda E, tt=tt: E.matmul(SSQ[:, tt * 128:(tt + 1) * 128], onesf[:, :], diag[:, tt * 128:(tt + 1) * 128],
                                                 start=True, stop=True, skip_group_check=True),
                 reads=[("diag", tt), "onesf"], writes=["SSQ"])

    def pre_norm(gi):
        norm_stats_from(lambda k: hT[:, k, :], lambda k: [("hT", k)])
        for k in range(8):
            S.op("dve", lambda E, k=k: E.scalar_tensor_tensor(xnT[:, k, :], hT[:, k, :], gsc[:, 8 * gi + k:8 * gi + k + 1], SSQ[:, :],
                                                              ALU.mult, ALU.mult),
                 reads=[("hT", k), "SSQ", "gsc"], writes=[("xnT", k)])

    def post_update(gi):
        for m in range(8):
            S.op("dve", lambda E, m=m: E.tensor_tensor(fT[:, m, :], fT[:, m, :], SSQ[:, :], ALU.mult),
                 reads=[("fT", m), "SSQ"], writes=[("fT", m)])
        for m in range(8):
            S.op("dve", lambda E, m=m: E.scalar_tensor_tensor(hT[:, m, :], fT[:, m, :], gsc[:, 8 * gi + m:8 * gi + m + 1], hT[:, m, :],
                                                              ALU.mult, ALU.add),
                 reads=[("fT", m), ("hT", m), "gsc"], writes=[("hT", m)])

    def ssq_accum(m, si):
        for tt in range(4):
            S.op("pe", lambda E, m=m, si=si, tt=tt: E.matmul(DEN[:, tt:tt + 1], sq[:, si, tt * 128:(tt + 1) * 128], onesb[:, 0:1],
                                                             start=(m == 0 and tt == 0), stop=False, skip_group_check=True),
                 reads=[("sq", si), "onesb"], writes=["DEN"])

    def load_x(c):
        for tt in range(4):
            r0 = c * T + tt * 128
            S.op("sp", lambda E, tt=tt, r0=r0: E.dma_start(out=xs[:, tt, :], in_=x_d[r0:r0 + 128, :]),
                 writes=[("fT", 2 * tt), ("fT", 2 * tt + 1)], dma=("x", tt))
        for k in range(8):
            b = next_pb()
            for tt in range(4):
                S.op("pe", lambda E, k=k, tt=tt, b=b: E.transpose(PB[b][:, tt * 128:(tt + 1) * 128], xs[:, tt, k * 128:(k + 1) * 128], identf[:, :]),
                     reads=[("fT", 2 * tt), ("fT", 2 * tt + 1), "identf"], writes=[("pb", b)])
            eng = "dve" if k % 2 == 0 else "act"
            if eng == "dve":
                S.op("dve", lambda E, k=k, b=b: E.tensor_copy(hT[:, k, :], PB[b][:, :]), reads=[("pb", b)], writes=[("hT", k)])
            else:
                S.op("act", lambda E, k=k, b=b: E.activation(hT[:, k, :], PB[b][:, :], AF.Identity), reads=[("pb", b)], writes=[("hT", k)])

    def store_out(c):
        for tt in range(4):
            for half in range(2):
                b = next_pb()
                for kk in range(4):
                    k = half * 4 + kk
                    S.op("pe", lambda E, k=k, kk=kk, tt=tt, b=b: E.transpose(PB[b][:, kk * 128:(kk + 1) * 128], hT[:, k, tt * 128:(tt + 1) * 128], identf[:, :]),
                         reads=[("hT", k), "identf"], writes=[("pb", b)])
                if half == 0:
                    S.op("dve", lambda E, tt=tt, b=b: E.tensor_copy(xs[:, tt, 0:512], PB[b][:, :]), reads=[("pb", b)], writes=[("fT", 2 * tt)])
                else:
                    S.op("act", lambda E, tt=tt, b=b: E.activation(xs[:, tt, 512:1024], PB[b][:, :], AF.Identity), reads=[("pb", b)], writes=[("fT", 2 * tt + 1)])
            r0 = c * T + tt * 128
            S.op("sp", lambda E, tt=tt, r0=r0: E.dma_start(out=out_d[r0:r0 + 128, :], in_=xs[:, tt, :]),
                 reads=[("fT", 2 * tt), ("fT", 2 * tt + 1)], dma=("o", tt))

    def early_prenorm(c, gi):
        for tt in range(4):
            r0 = c * T + tt * 128
            S.op("sp", lambda E, r0=r0: E.dma_start(out=xin[:, :], in_=x_d[r0:r0 + 128, :]), writes=["xin"], dma="xin")
            S.op("act", lambda E, tt=tt: E.activation(xnb[:, :], xin[:, :], AF.Square, accum_out=ssx[:, tt:tt + 1]),
                 reads=["xin"], writes=["xnb", ("ssx", tt)])
            S.op("dve", lambda E, tt=tt: E.tensor_scalar(tsx[:, tt:tt + 1], ssx[:, tt:tt + 1], float(D * EPS), None, ALU.add),
                 reads=[("ssx", tt)], writes=[("tsx", tt)])
            S.op("pool", lambda E, tt=tt: E.tensor_tensor(rsx[:, tt:tt + 1], tsx[:, tt:tt + 1], neghalf[:, 0:1], ALU.pow),
                 reads=[("tsx", tt), "neghalf"], writes=[("rsx", tt)])
            S.op("dve", lambda E, tt=tt: E.tensor_scalar(xnb[:, :], xin[:, :], rsx[:, tt:tt + 1], None, ALU.mult),
                 reads=["xin", ("rsx", tt)], writes=["xnb"])
            b = next_pb()
            for k in range(8):
                S.op("pe", lambda E, k=k, b=b: E.transpose(PBH[b][:, k * 128:(k + 1) * 128], xnb[:, k * 128:(k + 1) * 128], identb[:, :]),
                     reads=["xnb", "identb"], writes=[("pb", b)])
            S.op("dve", lambda E, tt=tt, b=b: E.tensor_tensor(xnT[:, :, tt * 128:(tt + 1) * 128],
                                                             PBH[b][:, :].rearrange("p (k t) -> p k t", k=8),
                                                             gsc[:, 8 * gi:8 * gi + 8].unsqueeze(2).broadcast_to([128, 8, 128]), ALU.mult),
                 reads=[("pb", b), "gsc"], writes=[("xnT", k) for k in range(8)])

    def ffn(which, g_pre, g_post, skip_prenorm=False, hooks=None):
        hooks = hooks or {}
        if not skip_prenorm:
            pre_norm(g_pre)
        def gu_evac(j, bg, bu):
            S.op("act", lambda E, bg=bg: E.activation(tmpA[:, :], PB[bg][:, :], AF.Silu), reads=[("pb", bg)], writes=["tmpA"])
            S.op("dve", lambda E, j=j, bu=bu: E.tensor_tensor(hid[:, j, :], tmpA[:, :], PB[bu][:, :], ALU.mult),
                 reads=["tmpA", ("pb", bu)], writes=[("hid", j)])
        s0 = use_block("gu")
        s1 = use_block("gu")
        wv0 = slot_view(s0, 8, 256)
        wv1 = slot_view(s1, 8, 256)
        fb = [next_pb() for _ in range(4)]
        for k in range(8):
            for gi_, (wv_, s_, off_) in enumerate([(wv0, s0, 0), (wv0, s0, 128), (wv1, s1, 0), (wv1, s1, 128)]):
                S.op("pe", lambda E, k=k, wv_=wv_, off_=off_, b=fb[gi_]: E.matmul(PB[b][:, :], wv_[:, k, off_:off_ + 128], xnT[:, k, :], start=(k == 0), stop=(k == 7)),
                     reads=[("w", s_), ("xnT", k)], writes=[("pb", fb[gi_])])
        issue_block()
        issue_block()
        gu_evac(0, fb[0], fb[1])
        gu_evac(1, fb[2], fb[3])
        if 1 in hooks:
            hooks[1]()
        for j in range(2, NHT):
            s = use_block("gu")
            wv = slot_view(s, 8, 256)
            bg = next_pb()
            bu = next_pb()
            for k in range(8):
                S.op("pe", lambda E, k=k, wv=wv, bg=bg: E.matmul(PB[bg][:, :], wv[:, k, 0:128], xnT[:, k, :], start=(k == 0), stop=(k == 7)),
                     reads=[("w", s), ("xnT", k)], writes=[("pb", bg)])
            for k in range(8):
                S.op("pe", lambda E, k=k, wv=wv, bu=bu: E.matmul(PB[bu][:, :], wv[:, k, 128:256], xnT[:, k, :], start=(k == 0), stop=(k == 7)),
                     reads=[("w", s), ("xnT", k)], writes=[("pb", bu)])
            issue_block()
            gu_evac(j, bg, bu)
            if j in hooks:
                hooks[j]()
        pend = []
        for mp in range(4):
            b0 = next_pb()
            b1 = next_pb()
            bb = [b0, b1]
            for kh in range(2):
                s = use_block("dn")
                wv = slot_view(s, 11, 256)
                for mi in range(2):
                    for kk in range(11):
                        kg = kh * 11 + kk
                        S.op("pe", lambda E, wv=wv, mi=mi, kk=kk, kg=kg, b=bb[mi]: E.matmul(PB[b][:, :], wv[:, kk, mi * 128:(mi + 1) * 128], hid[:, kg, :],
                                                                                     start=(kg == 0), stop=(kg == NHT - 1)),
                             reads=[("w", s), ("hid", kg)], writes=[("pb", bb[mi])])
                issue_block()
            for p_ in pend:
                ssq_accum(*p_)
            pend = []
            for mi in range(2):
                m = 2 * mp + mi
                si = state["sq"]
                state["sq"] ^= 1
                S.op("dve", lambda E, m=m, b=bb[mi]: E.tensor_copy(fT[:, m, :], PB[b][:, :]), reads=[("pb", bb[mi])], writes=[("fT", m)])
                S.op("act", lambda E, si=si, m=m: E.activation(sq[:, si, :], fT[:, m, :], AF.Square), reads=[("fT", m)], writes=[("sq", si)])
                pend.append((m, si))
        for p_ in pend:
            ssq_accum(*p_)
        finish_rstd(D * EPS)
        post_update(g_post)

    def in_proj(c):
        pre_norm(G_ATT_PRE)

        def evac_tile(tile, b):
            bias = vecs[:, NV_BIN + tile:NV_BIN + tile + 1]
            if tile < 4 or 6 <= tile < 10:
                for hf in range(2):
                    rows = slice(64 * hf, 64 * hf + 64)
                    if tile < 4:
                        dst = QaP[rows, hf, tile, :]
                        wk = ["QaP"]
                    else:
                        dst = QbD[rows, tile - 6, hf, :]
                        wk = ["QbD"]
                    dstv = dst
                    srcv = PB[b][rows, :]
                    if tile % 2 == 0:
                        S.op("act", lambda E, dstv=dstv, srcv=srcv, tile=tile, rows=rows: E.activation(dstv, srcv, AF.Identity, bias=vecs[rows, NV_BIN + tile:NV_BIN + tile + 1]),
                             reads=[("pb", b), "vecs"], writes=wk)
                    else:
                        S.op("dve", lambda E, dstv=dstv, srcv=srcv, tile=tile, rows=rows: E.tensor_scalar(dstv, srcv, vecs[rows, NV_BIN + tile:NV_BIN + tile + 1], None, ALU.add),
                             reads=[("pb", b), "vecs"], writes=wk)
            else:
                if tile == 4:
                    dst, wk = KaT[:, (c % 2) * T:(c % 2) * T + T], ["KaT"]
                elif tile == 5:
                    dst, wk = VaT[:, (c % 2) * T:(c % 2) * T + T], ["VaT"]
                elif tile < 14:
                    dst, wk = KbT[:, tile - 10, c * T:(c + 1) * T], ["KbT"]
                else:
                    dst, wk = VbT[:, tile - 14, c * T:(c + 1) * T], ["VbT"]
                if tile % 2 == 0:
                    S.op("act", lambda E, dst=dst, b=b, bias=bias: E.activation(dst, PB[b][:, :], AF.Identity, bias=bias),
                         reads=[("pb", b), "vecs"], writes=wk)
                else:
                    S.op("dve", lambda E, dst=dst, b=b, bias=bias: E.tensor_scalar(dst, PB[b][:, :], bias, None, ALU.add),
                         reads=[("pb", b), "vecs"], writes=wk)

        s0 = use_block("in")
        s1 = use_block("in")
        wvs = [(slot_view(s0, 8, 256), s0), (slot_view(s1, 8, 256), s1)]
        fb = [next_pb() for _ in range(4)]
        for k in range(8):
            for t4 in range(4):
                wv_, s_ = wvs[t4 // 2]
                ti = t4 % 2
                S.op("pe", lambda E, k=k, wv_=wv_, ti=ti, b=fb[t4]: E.matmul(PB[b][:, :], wv_[:, k, ti * 128:(ti + 1) * 128], xnT[:, k, :], start=(k == 0), stop=(k == 7)),
                     reads=[("w", s_), ("xnT", k)], writes=[("pb", fb[t4])])
        issue_block()
        issue_block()
        for t4 in range(4):
            evac_tile(t4, fb[t4])
        for cb in range(2, 9):
            s = use_block("in")
            wv = slot_view(s, 8, 256)
            for ti in range(2):
                tile = 2 * cb + ti
                b = next_pb()
                for k in range(8):
                    S.op("pe", lambda E, k=k, wv=wv, ti=ti, b=b: E.matmul(PB[b][:, :], wv[:, k, ti * 128:(ti + 1) * 128], xnT[:, k, :], start=(k == 0), stop=(k == 7)),
                         reads=[("w", s), ("xnT", k)], writes=[("pb", b)])
                evac_tile(tile, b)
            issue_block()

    vt_state = {"n": None}

    def attention(c):
        def a_blk_off(bk):
            return 128 * (bk % 8)
        abks = [4 * c + j for j in range(-1, 4) if 4 * c + j >= 0]
        vta = {}
        nvt = 0
        S.alias_barrier([("fT", i) for i in range(8)], [("Vt", i) for i in range(64)])

        def flush_vt(batch, rkeys):
            n = len(batch)
            base = batch[0][0]
            ph = vt_state["n"]
            vt_state["n"] = None
            if False:
                S.op("act", lambda E, n=n, base=base, ph=ph: E.activation(Vt[:, base:base + n, :], PBH[ph][:, 0:n * 128].rearrange("p (n f) -> p n f", f=128), AF.Identity),
                     reads=[("pb", ph)], writes=[("Vt", i) for i, _ in batch])
            else:
                S.op("dve", lambda E, n=n, base=base, ph=ph: E.tensor_copy(Vt[:, base:base + n, :], PBH[ph][:, 0:n * 128].rearrange("p (n f) -> p n f", f=128)),
                     reads=[("pb", ph)], writes=[("Vt", i) for i, _ in batch])

        def vt_bank():
            if vt_state["n"] is None:
                vt_state["n"] = next_pb()
            return vt_state["n"]
        batch = []
        for bk in abks:
            vta[bk] = nvt
            off = a_blk_off(bk)
            slot = len(batch)
            ph = vt_bank()
            S.op("pe", lambda E, off=off, slot=slot, ph=ph: E.transpose(PBH[ph][:, slot * 128:(slot + 1) * 128], VaT[:, off:off + 128], identb[:, :]),
                 reads=["VaT", "identb"], writes=[("pb", ph)])
            batch.append((nvt, bk))
            nvt += 1
            if len(batch) == 4:
                flush_vt(batch, None)
                batch = []
        if batch:
            flush_vt(batch, None)
            batch = []
        pslot = {}
        ns = 0
        for j in range(4):
            for g in range(2):
                for blk in range(2):
                    bk = 4 * c + j - (1 - blk)
                    if bk < 0:
                        continue
                    b = next_pb()
                    off = a_blk_off(bk)
                    rv = QaP[:, g, :, 128 * j:128 * j + 128]
                    ov = PB[b][:, :]
                    ecol = E_A + (g * 2 + blk) * 512
                    S.op("pe", lambda E, off=off, rv=rv, ov=ov: E.matmul(ov, KaT[:, off:off + 128], rv, start=True, stop=False),
                         reads=["KaT", "QaP"], writes=[("pb", b)])
                    S.op("pe", lambda E, ov=ov, ecol=ecol: E.matmul(ov, identb[:, :], Ebf[:, ecol:ecol + 512], start=False, stop=True),
                         reads=["Ebf", "identb"], writes=[("pb", b)])
                    sl = ns
                    ns += 1
                    pslot[(j, g, blk)] = sl
                    S.op("act", lambda E, sl=sl, b=b: E.activation(hid[:, sl, :], PB[b][:, :], AF.Exp, scale=0.125),
                         reads=[("pb", b)], writes=[("hid", sl)])
        for hh in range(4):
            (NUMb, NK), (DENb, DK) = NUMS[hh % 2], DENS[hh % 2]
            first = [True, True]
            for j in range(4):
                for blk in range(2):
                    bk = 4 * c + j - (1 - blk)
                    if bk < 0:
                        continue
                    vt = vta[bk]
                    for hf in range(2):
                        rows = slice(64 * hf, 64 * hf + 64)
                        sl = pslot[(j, hf, blk)]
                        rhs = hid[:, sl, hh * 128:(hh + 1) * 128]
                        on = NUMb[rows, 128 * j:128 * j + 128]
                        od = DENb[rows, 128 * j:128 * j + 128]
                        st = first[hf]
                        first[hf] = False
                        S.op("pe", lambda E, on=on, vt=vt, hf=hf, rhs=rhs, st=st: E.matmul(on, Vt[:, vt, 64 * hf:64 * hf + 64], rhs, start=st, stop=False, skip_group_check=True),
                             reads=[("Vt", vt), ("hid", sl)], writes=[NK])
                        S.op("pe", lambda E, od=od, hf=hf, rhs=rhs, st=st: E.matmul(od, onesb[:, 0:64], rhs, start=st, stop=False, skip_group_check=True),
                             reads=[("hid", sl), "onesb"], writes=[DK])
            S.op("act", lambda E, hh=hh, DENb=DENb: E.activation(rden[:, :], DENb[:, :], AF.Ln, bias=esink[:, hh:hh + 1]), reads=[DK, "esink"], writes=["rden"])
            S.op("act", lambda E: E.activation(rden[:, :], rden[:, :], AF.Exp, scale=-1.0), reads=["rden"], writes=["rden"])
            S.op("dve", lambda E, hh=hh, NUMb=NUMb: E.tensor_tensor(xnT[:, hh, :], NUMb[:, :], rden[:, :], ALU.mult),
                 reads=[NK, "rden"], writes=[("xnT", hh)])
        cm = c % 4
        k1 = c // 4
        for pr in range(4):
            (NUMb, NK), (DENb, DK) = NUMS[pr % 2], DENS[pr % 2]
            vtb = {}
            nvt = 8
            batch = []
            def add_vt(key, ap):
                nonlocal nvt, batch
                slot = len(batch)
                vtb[key] = nvt
                ph = vt_bank()
                S.op("pe", lambda E, ap=ap, slot=slot, ph=ph: E.transpose(PBH[ph][:, slot * 128:(slot + 1) * 128], ap, identb[:, :]),
                     reads=["VbT", "identb"], writes=[("pb", ph)])
                batch.append((nvt, key))
                nvt += 1
                if len(batch) == 4:
                    flush_vt(batch, None)
                    batch = []
            for bk in abks:
                add_vt(("d1", bk), VbT[:, pr, 128 * bk:128 * bk + 128])
            for r4 in range(4):
                for cb_ in (c - 1, c):
                    if cb_ < 0:
                        continue
                    st_ = 512 * cb_ + r4
                    add_vt(("d4", r4, cb_), VbT[:, pr, st_:st_ + 509:4])
            for r in range(16):
                for kb in (k1 - 1, k1):
                    if kb < 0:
                        continue
                    st_ = 2048 * kb + r
                    add_vt(("d16", r, kb), VbT[:, pr, st_:st_ + 2033:16])
            if batch:
                flush_vt(batch, None)
                batch = []
            ps_ = {}
            ns = 0
            for j in range(4):
                b = next_pb()
                sl = ns
                ns += 1
                used = []
                for blk in range(2):
                    bk = 4 * c + j - (1 - blk)
                    if bk < 0:
                        continue
                    used.append(blk)
                    rv = QbD[:, pr, :, 128 * j:128 * j + 128]
                    ov = PB[b][:, blk * 256:(blk + 1) * 256]
                    ecb = E_1 + pr * 512 + blk * 256
                    S.op("pe", lambda E, bk=bk, rv=rv, ov=ov, pr=pr: E.matmul(ov, KbT[:, pr, 128 * bk:128 * bk + 128], rv, start=True, stop=False),
                         reads=["KbT", "QbD"], writes=[("pb", b)])
                    S.op("pe", lambda E, ov=ov, ecb=ecb: E.matmul(ov, identb[:, :], Ebf[:, ecb:ecb + 256], start=False, stop=True),
                         reads=["Ebf", "identb"], writes=[("pb", b)])
                    ps_[("d1", j, blk)] = (sl, blk * 256)
                c0 = used[0] * 256
                c1 = (used[-1] + 1) * 256
                ecol = E_1 + pr * 512
                S.op("act", lambda E, sl=sl, b=b, c0=c0, c1=c1: E.activation(hid[:, sl, c0:c1], PB[b][:, c0:c1], AF.Exp, scale=0.125),
                     reads=[("pb", b)], writes=[("hid", sl)])
            for r4 in range(4):
                b = next_pb()
                sl = ns
                ns += 1
                used = []
                for blk in range(2):
                    cb_ = c - (1 - blk)
                    if cb_ < 0:
                        continue
                    used.append(blk)
                    st_ = 512 * cb_ + r4
                    rv = QbD[:, pr, :, r4:512:4]
                    ov = PB[b][:, blk * 256:(blk + 1) * 256]
                    ecb = E_4 + pr * 512 + blk * 256
                    S.op("pe", lambda E, st_=st_, rv=rv, ov=ov, pr=pr: E.matmul(ov, KbT[:, pr, st_:st_ + 509:4], rv, start=True, stop=False),
                         reads=["KbT", "QbD"], writes=[("pb", b)])
                    S.op("pe", lambda E, ov=ov, ecb=ecb: E.matmul(ov, identb[:, :], Ebf[:, ecb:ecb + 256], start=False, stop=True),
                         reads=["Ebf", "identb"], writes=[("pb", b)])
                    ps_[("d4", r4, blk)] = (sl, blk * 256)
                c0 = used[0] * 256
                c1 = (used[-1] + 1) * 256
                ecol = E_4 + pr * 512
                S.op("act", lambda E, sl=sl, b=b, c0=c0, c1=c1: E.activation(hid[:, sl, c0:c1], PB[b][:, c0:c1], AF.Exp, scale=0.125),
                     reads=[("pb", b)], writes=[("hid", sl)])
            for blk in range(2):
                kb = k1 - (1 - blk)
                if kb < 0:
                    continue
                for rh in range(2):
                    b = next_pb()
                    sl = ns
                    ns += 1
                    for rr in range(8):
                        r = rh * 8 + rr
                        st_ = 2048 * kb + r
                        rv = QbD[:, pr, :, r:512:16]
                        ov = PB[b][:, rr * 64:(rr + 1) * 64]
                        ecol = E_16 + pr * 512 + (cm * 2 + blk) * 64
                        S.op("pe", lambda E, st_=st_, rv=rv, ov=ov, pr=pr: E.matmul(ov, KbT[:, pr, st_:st_ + 2033:16], rv, start=True, stop=False),
                             reads=["KbT", "QbD"], writes=[("pb", b)])
                        S.op("pe", lambda E, ov=ov, ecol=ecol: E.matmul(ov, identb[:, :], Ebf[:, ecol:ecol + 64], start=False, stop=True),
                             reads=["Ebf", "identb"], writes=[("pb", b)])
                        ps_[("d16", r, blk)] = (sl, rr * 64)
                    S.op("act", lambda E, sl=sl, b=b: E.activation(hid[:, sl, :], PB[b][:, :], AF.Exp, scale=0.125),
                         reads=[("pb", b)], writes=[("hid", sl)])
            first = [True, True]

            def pv(vt, sl, col, width, out_num, out_den, rhs_view):
                for eo in range(2):
                    rows = slice(64 * eo, 64 * eo + 64)
                    rhs = rhs_view(hid[:, sl, col + eo * width:col + (eo + 1) * width])
                    st = first[eo]
                    first[eo] = False
                    on = out_num(rows)
                    od = out_den(rows)
                    S.op("pe", lambda E, on=on, vt=vt, eo=eo, rhs=rhs, st=st: E.matmul(on, Vt[:, vt, 64 * eo:64 * eo + 64], rhs, start=st, stop=False, skip_group_check=True),
                         reads=[("Vt", vt), ("hid", sl)], writes=[NK])
                    S.op("pe", lambda E, od=od, rhs=rhs, st=st: E.matmul(od, onesb[:, 0:64], rhs, start=st, stop=False, skip_group_check=True),
                         reads=[("hid", sl), "onesb"], writes=[DK])
            for j in range(4):
                for blk in range(2):
                    bk = 4 * c + j - (1 - blk)
                    if bk < 0:
                        continue
                    sl, col = ps_[("d1", j, blk)]
                    pv(vtb[("d1", bk)], sl, col, 128,
                       lambda rows, j=j, NUMb=NUMb: NUMb[rows, 128 * j:128 * j + 128],
                       lambda rows, j=j, DENb=DENb: DENb[rows, 128 * j:128 * j + 128],
                       lambda ap: ap)
            for r4 in range(4):
                for blk in range(2):
                    cb_ = c - (1 - blk)
                    if cb_ < 0:
                        continue
                    sl, col = ps_[("d4", r4, blk)]
                    pv(vtb[("d4", r4, cb_)], sl, col, 128,
                       lambda rows, r4=r4, NUMb=NUMb: NUMb[rows, r4:512:4],
                       lambda rows, r4=r4, DENb=DENb: DENb[rows, r4:512:4],
                       lambda ap: ap)
            for r in range(16):
                for blk in range(2):
                    kb = k1 - (1 - blk)
                    if kb < 0:
                        continue
                    sl, col = ps_[("d16", r, blk)]
                    pv(vtb[("d16", r, kb)], sl, col, 32,
                       lambda rows, r=r, NUMb=NUMb: NUMb[rows, r:512:16],
                       lambda rows, r=r, DENb=DENb: DENb[rows, r:512:16],
                       lambda ap: ap)
            S.op("act", lambda E, DENb=DENb: E.activation(rden[:, :], DENb[:, :], AF.Ln), reads=[DK], writes=["rden"])
            S.op("act", lambda E: E.activation(rden[:, :], rden[:, :], AF.Exp, scale=-1.0), reads=["rden"], writes=["rden"])
            S.op("dve", lambda E, pr=pr, NUMb=NUMb: E.tensor_tensor(xnT[:, 4 + pr, :], NUMb[:, :], rden[:, :], ALU.mult),
                 reads=[NK, "rden"], writes=[("xnT", 4 + pr)])
        S.alias_barrier([("Vt", i) for i in range(64)], [("fT", i) for i in range(8)])

    def out_proj():
        pend = []
        for cb in range(4):
            s = use_block("out")
            wv = slot_view(s, 8, 256)
            for ti in range(2):
                m = 2 * cb + ti
                b = next_pb()
                for k in range(8):
                    S.op("pe", lambda E, k=k, wv=wv, ti=ti, b=b: E.matmul(PB[b][:, :], wv[:, k, ti * 128:(ti + 1) * 128], xnT[:, k, :], start=(k == 0), stop=(k == 7)),
                         reads=[("w", s), ("xnT", k)], writes=[("pb", b)])
                bias = vecs[:, NV_BOUT + m:NV_BOUT + m + 1]
                si = state["sq"]
                state["sq"] ^= 1
                S.op("dve", lambda E, m=m, b=b, bias=bias: E.tensor_scalar(fT[:, m, :], PB[b][:, :], bias, None, ALU.add),
                     reads=[("pb", b), "vecs"], writes=[("fT", m)])
                S.op("act", lambda E, si=si, m=m: E.activation(sq[:, si, :], fT[:, m, :], AF.Square),
                     reads=[("fT", m)], writes=[("sq", si)])
                for p_ in pend:
                    ssq_accum(*p_)
                pend = [(m, si)]
            issue_block()
        for p_ in pend:
            ssq_accum(*p_)
        finish_rstd(D * EPS)
        post_update(G_ATT_POST)

    def ple(c):
        for tt in range(4):
            r0 = c * T + tt * 128
            S.op("pool", lambda E, tt=tt, r0=r0: E.dma_start(out=pin[:, tt, :], in_=p_d[r0:r0 + 128, :]),
                 writes=[("pin", tt)], dma=("p", tt))
        for k2 in range(2):
            bt = next_pb()
            for tt in range(4):
                S.op("pe", lambda E, k2=k2, tt=tt, bt=bt: E.transpose(PBH[bt][:, tt * 128:(tt + 1) * 128], pin[:, tt, k2 * 128:(k2 + 1) * 128], identb[:, :]),
                     reads=[("pin", tt), "identb"], writes=[("pb", bt)])
            S.op("act", lambda E, k2=k2, bt=bt: E.activation(pT[:, k2, :], PBH[bt][:, 0:512], AF.Identity), reads=[("pb", bt)], writes=[("pT", k2)])
        pre_norm(G_PLE_PRE)
        sp_ = None
        gate_slots = []
        pend = []
        for cb in range(4):
            s = use_block("gate")
            gate_slots.append(s)
            wv = slot_view(s, 8, 256)
            if cb == 0:
                sp_ = None
            for ti in range(2):
                m = 2 * cb + ti
                bgt = next_pb()
                for k in range(8):
                    S.op("pe", lambda E, k=k, wv=wv, ti=ti, b=bgt: E.matmul(PB[b][:, :], wv[:, k, ti * 128:(ti + 1) * 128], xnT[:, k, :], start=(k == 0), stop=(k == 7)),
                         reads=[("w", s), ("xnT", k)], writes=[("pb", bgt)])
                S.op("act", lambda E, m=m, b=bgt: E.activation(fT[:, m, :], PB[b][:, :], AF.Tanh, scale=0.5),
                     reads=[("pb", bgt)], writes=[("fT", m)])
            issue_block()
        s = use_block("proj")
        wv = slot_view(s, 2, 1024)
        for m in range(8):
            be = next_pb()
            for k2 in range(2):
                S.op("pe", lambda E, k2=k2, wv=wv, m=m, b=be: E.matmul(PB[b][:, :], wv[:, k2, m * 128:(m + 1) * 128], pT[:, k2, :], start=(k2 == 0), stop=(k2 == 1)),
                     reads=[("w", s), ("pT", k2)], writes=[("pb", be)])
            S.op("dve", lambda E, m=m, b=be: E.scalar_tensor_tensor(fT[:, m, :], fT[:, m, :], 1.0, PB[b][:, :], ALU.add, ALU.mult),
                 reads=[("fT", m), ("pb", be)], writes=[("fT", m)])
            si = state["sq"]
            state["sq"] ^= 1
            S.op("act", lambda E, m=m, si=si: E.activation(sq[:, si, :], fT[:, m, :], AF.Square), reads=[("fT", m)], writes=[("sq", si)])
            for p_ in pend:
                ssq_accum(*p_)
            pend = [(m, si)]
        issue_block()
        for p_ in pend:
            ssq_accum(*p_)

    def ple_tail():
        finish_rstd(4.0 * D * EPS)
        post_update(G_PLE_POST)

    full = debug_stage is None
    for c in range(nchunks):
        if full:
            if c == 0:
                early_prenorm(0, G_FFN1_PRE)
                hooks = {1: (lambda: load_x(0))}
            else:
                hooks = {1: ple_tail, 4: (lambda c=c: store_out(c - 1)), 7: (lambda c=c: load_x(c))}
            ffn(0, G_FFN1_PRE, G_FFN1_POST, skip_prenorm=True, hooks=hooks)
        else:
            load_x(c)
            if debug_stage == "x":
                store_out(c)
                continue
            ffn(0, G_FFN1_PRE, G_FFN1_POST)
        if debug_stage != "ffn1":
            in_proj(c)
            attention(c)
            if debug_stage == "mix" and c == nchunks - 1:
                dbg = {}
                import os
                only = os.environ.get("DBG_DUMPS", "")
                def dump(name, ap, n, keys):
                    if only and name not in only.split(","):
                        return
                    d = nc.dram_tensor(name, [128, n], BF16, kind="ExternalOutput").ap()
                    S.op("sp", lambda E, d=d, ap=ap: E.dma_start(out=d[:, :], in_=ap), reads=keys, dma="dbg_" + name)
                    dbg_keys.extend(keys)
                dump("dbg_mix", xnT[:, :, :].rearrange("p k t -> p (k t)"), 4096, [("xnT", k) for k in range(8)])
                dump("dbg_qa", QaP[:, :, :, :].rearrange("p a b t -> p (a b t)"), 4096, ["QaP"])
                dump("dbg_qb", QbD[:, :, :, :].rearrange("p a b t -> p (a b t)"), 4096, ["QbD"])
                dump("dbg_ka", KaT[:, :], 1024, ["KaT"])
                dump("dbg_va", VaT[:, :], 1024, ["VaT"])
                for pr in range(4):
                    dump(f"dbg_kb{pr}", KbT[:, pr, 0:512], 512, ["KbT"])
                    dump(f"dbg_vb{pr}", VbT[:, pr, 0:512], 512, ["VbT"])
            out_proj()
            if debug_stage not in ("attn", "mix"):
                ffn(1, G_FFN2_PRE, G_FFN2_POST)
                if debug_stage != "ffn2":
                    ple(c)
                    if full:
                        if c + 1 < nchunks:
                            early_prenorm(c + 1, G_FFN1_PRE)
                        else:
                            ple_tail()
                            store_out(c)
                        continue
                    ple_tail()
        store_out(c)
    S.op("sp", lambda E: None, reads=[("fT", i) for i in range(8)], writes=[("fT", i) for i in range(8)] + dbg_keys)

    S.finalize()
    sem_ids = list(Sched.ENGS) + S.dma_keys
    sems = {}
    for i, sid in enumerate(sem_ids):
        sems[sid] = es.enter_context(nc.semaphore(f"s{i}"))
    block = es.enter_context(nc.Block())
    S.emit(nc, block, sems)
    es.close()
    return nc


_CACHE = {}


def kernel(**inputs):
    shared = _prep_shared(inputs)
    x = np.asarray(inputs["x"], dtype=np.float32)
    p = np.asarray(inputs["p"], dtype=np.float32)
    in_maps = []
    for b in range(NB):
        m = dict(shared)
        m["x"] = np.ascontiguousarray(x[b])
        m["p"] = np.ascontiguousarray(p[0, b])
        in_maps.append(m)
    nc = build_program()
    res = run_bass_kernel_spmd(nc, in_maps, core_ids=list(range(NB)))
    out = np.stack([np.asarray(res.results[b]["out"], dtype=np.float32) for b in range(NB)], axis=0)
    return out
```

```python
import numpy as np
import concourse.bass as bass
import concourse.mybir as mybir
from concourse.bass_utils import run_bass_kernel_spmd

F32 = mybir.dt.float32
BF16 = mybir.dt.bfloat16
AF = mybir.ActivationFunctionType
ALU = mybir.AluOpType

D = 1024
SEQ = 4096
NB = 8
T = 512
NCH = SEQ // T
DFF = 2816
NHT = DFF // 128
DIN = 2304
PLE = 256
EPS = 1e-6
NEG = -30000.0
NSLOT = 4
SLOT_ELEMS = 2816

E_A, E_1, E_4, E_16 = 0, 2048, 4096, 6144
NV_G = 0
NV_BIN = 64
NV_BOUT = 82
NV_SINK = 90
NV = 94
G_FFN1_PRE, G_FFN1_POST, G_ATT_PRE, G_ATT_POST, G_FFN2_PRE, G_FFN2_POST, G_PLE_PRE, G_PLE_POST = range(8)

DEBUG_STAGE = None


class _Op:
    __slots__ = ("eng", "fn", "deps", "dma", "signal", "val")


class Sched:
    ENGS = ("pe", "act", "dve", "pool", "sp")
    STRICT = True

    def __init__(self):
        self.ops = {e: [] for e in self.ENGS}
        self.last_w = {}
        self.readers = {}
        self.dma_keys = []
        self.extra = {}

    def alias_barrier(self, from_keys, to_keys):
        deps = {}
        for k in from_keys:
            w = self.last_w.get(k)
            if w is not None:
                deps[id(w)] = w
            for r in self.readers.get(k, ()):
                deps[id(r)] = r
        for k in to_keys:
            self.extra.setdefault(k, {}).update(deps)

    def op(self, eng, fn, reads=(), writes=(), dma=None):
        o = _Op()
        o.eng, o.fn, o.dma, o.signal, o.val = eng, fn, dma, False, 0
        deps = {}
        for k in list(reads) + list(writes):
            ex = self.extra.pop(k, None)
            if ex:
                deps.update(ex)
        same = {}
        for k in reads:
            w = self.last_w.get(k)
            if w is not None:
                deps[id(w)] = w
                same[id(w)] = True
        for k in writes:
            w = self.last_w.get(k)
            if w is not None:
                deps[id(w)] = w
                same[id(w)] = True
            for r in self.readers.get(k, ()):
                deps[id(r)] = r
        o.deps = [d for d in deps.values()
                  if d is not o and (d.dma is not None or d.eng != eng or (self.STRICT and eng != "pe"))]
        for k in reads:
            self.readers.setdefault(k, []).append(o)
        for k in writes:
            self.last_w[k] = o
            self.readers[k] = []
        if dma is not None and dma not in self.dma_keys:
            self.dma_keys.append(dma)
        self.ops[eng].append(o)
        return o

    def finalize(self):
        for e in self.ENGS:
            for o in self.ops[e]:
                for d in o.deps:
                    d.signal = True
        dcount = {}
        for e in self.ENGS:
            c = 0
            for o in self.ops[e]:
                if o.dma is not None:
                    pass
                elif o.signal:
                    c += 1
                    o.val = c
        for e in self.ENGS:
            for o in self.ops[e]:
                if o.dma is not None:
                    dcount[o.dma] = dcount.get(o.dma, 0) + 16
                    o.val = dcount[o.dma]

    def emit(self, nc, block, sems):
        def make(eng_name):
            def body(E):
                waited = {}
                for o in self.ops[eng_name]:
                    need = {}
                    for d in o.deps:
                        sid = d.dma if d.dma is not None else d.eng
                        if d.val > need.get(sid, 0):
                            need[sid] = d.val
                    for sid, v in need.items():
                        if waited.get(sid, 0) >= v:
                            continue
                        E.wait_ge(sems[sid], v)
                        waited[sid] = v
                    ins = o.fn(E)
                    if ins is None:
                        continue
                    if o.dma is not None:
                        ins.then_inc(sems[o.dma], 16)
                    elif o.signal:
                        ins.then_inc(sems[o.eng], 1)
            return body
        block.tensor(make("pe"))
        block.scalar(make("act"))
        block.vector(make("dve"))
        block.gpsimd(make("pool"))
        block.sync(make("sp"))


def _t5_bucket(dist):
    n = np.maximum(dist, 0)
    nf = np.maximum(n, 1).astype(np.float64)
    large = 16 + (np.log(nf / 16.0) / np.log(2048.0 / 16.0) * 16.0).astype(np.int64)
    large = np.minimum(large, 31)
    return np.where(n < 16, n, large)


def _build_bias(rel_bias):
    rb = np.asarray(rel_bias, dtype=np.float32)
    E = np.full((128, 8192), NEG, dtype=np.float32)
    kk = np.arange(128)[:, None]
    i = np.arange(128)[None, :]
    u1 = i
    u4 = i
    for blk in range(2):
        off = 128 if blk == 0 else 0
        d1 = u1 - kk + off
        d4 = u4 - kk + off
        validA = (d1 >= 0) & (d1 <= 127)
        bA = _t5_bucket(d1)
        for g in range(2):
            for hh in range(4):
                h = hh + 4 * g
                col = E_A + (g * 2 + blk) * 512 + hh * 128
                E[:, col:col + 128] = np.where(validA, rb[bA, h], NEG)
        valid1 = (d1 >= 0) & (d1 <= 128)
        b1 = _t5_bucket(d1)
        valid4 = (d4 >= 0) & (d4 <= 128)
        b4 = _t5_bucket(d4 * 4)
        for pr in range(4):
            for eo in range(2):
                h = 8 + 2 * pr + eo
                col = E_1 + pr * 512 + blk * 256 + eo * 128
                E[:, col:col + 128] = np.where(valid1, rb[b1, h], NEG)
                col = E_4 + pr * 512 + blk * 256 + eo * 128
                E[:, col:col + 128] = np.where(valid4, rb[b4, h], NEG)
    i32 = np.arange(32)[None, :]
    for cm in range(4):
        for blk in range(2):
            off = 128 if blk == 0 else 0
            d16 = 32 * cm + i32 - kk + off
            valid = (d16 >= 0) & (d16 <= 128)
            b16 = _t5_bucket(d16 * 16)
            for pr in range(4):
                for eo in range(2):
                    h = 8 + 2 * pr + eo
                    col = E_16 + pr * 512 + (cm * 2 + blk) * 64 + eo * 32
                    E[:, col:col + 32] = np.where(valid, rb[b16, h], NEG)
    return E


_QA_HEAD_ORDER = [0, 4, 1, 5, 2, 6, 3, 7]


def _prep_shared(inp):
    f = lambda a: np.ascontiguousarray(np.asarray(a, dtype=np.float32))
    w_in = f(inp["w_in"][0])
    b_in = f(inp["b_in"][0])
    qcols = np.concatenate([np.arange(h * 64, h * 64 + 64) for h in _QA_HEAD_ORDER])
    perm = np.concatenate([qcols, np.arange(512, DIN)])
    w_in_p = np.ascontiguousarray(w_in[:, perm])
    b_in_p = b_in[perm]
    w_out = f(inp["w_out"][0])
    rperm = np.concatenate([qcols, np.arange(512, 1024)])
    w_out_p = np.ascontiguousarray(w_out[rperm, :])

    def gu_interleave(w):
        w = f(w)
        g = w[:, :DFF].reshape(D, NHT, 128)
        u = w[:, DFF:].reshape(D, NHT, 128)
        return np.ascontiguousarray(np.stack([g, u], axis=2).reshape(D, 2 * DFF))

    vecs = np.zeros((128, NV), dtype=np.float32)
    gains = [inp["ffn1_pre_g"], inp["ffn1_post_g"], inp["attn_pre_g"], inp["attn_post_g"],
             inp["ffn2_pre_g"], inp["ffn2_post_g"], inp["ple_pre_g"], inp["ple_post_g"]]
    for gi, g in enumerate(gains):
        vecs[:, NV_G + 8 * gi: NV_G + 8 * gi + 8] = f(g[0]).reshape(8, 128).T
    vecs[:, NV_BIN:NV_BIN + 18] = b_in_p.reshape(18, 128).T
    vecs[:, NV_BOUT:NV_BOUT + 8] = f(inp["b_out"][0]).reshape(8, 128).T
    sinks = f(inp["sinks"][0])
    for hh in range(4):
        vecs[0:64, NV_SINK + hh] = sinks[hh]
        vecs[64:128, NV_SINK + hh] = sinks[hh + 4]
    shared = {
        "w_gu1": gu_interleave(inp["ffn1_w_gu"][0]),
        "w_d1": f(inp["ffn1_w_down"][0]),
        "w_in": w_in_p,
        "w_out": w_out_p,
        "w_gu2": gu_interleave(inp["ffn2_w_gu"][0]),
        "w_d2": f(inp["ffn2_w_down"][0]),
        "w_gate": f(inp["w_ple_gate"][0]),
        "w_proj": f(inp["w_ple_proj"][0]),
        "vecs": vecs,
        "ebias": _build_bias(inp["rel_bias"]),
        "identf": np.eye(128, dtype=np.float32),
    }
    return shared


def build_program(nchunks=NCH, debug_stage=None):
    nc = bass.Bass("TRN2", target_bir_lowering=False)
    dt = lambda name, shape, kind: nc.dram_tensor(name, shape, F32, kind=kind).ap()
    x_d = dt("x", [SEQ, D], "ExternalInput")
    p_d = dt("p", [SEQ, PLE], "ExternalInput")
    wgu_d = [dt("w_gu1", [D, 2 * DFF], "ExternalInput"), dt("w_gu2", [D, 2 * DFF], "ExternalInput")]
    wd_d = [dt("w_d1", [DFF, D], "ExternalInput"), dt("w_d2", [DFF, D], "ExternalInput")]
    win_d = dt("w_in", [D, DIN], "ExternalInput")
    wout_d = dt("w_out", [D, D], "ExternalInput")
    wgate_d = dt("w_gate", [D, D], "ExternalInput")
    wproj_d = dt("w_proj", [PLE, D], "ExternalInput")
    vecs_d = dt("vecs", [128, NV], "ExternalInput")
    ebias_d = dt("ebias", [128, 8192], "ExternalInput")
    identf_d = dt("identf", [128, 128], "ExternalInput")
    out_d = dt("out", [SEQ, D], "ExternalOutput")

    S = Sched()
    from contextlib import ExitStack
    es = ExitStack()
    sb = lambda name, shape, dtype: es.enter_context(nc.sbuf_tensor(name, shape, dtype))
    ps = lambda name, shape, dtype: es.enter_context(nc.psum_tensor(name, shape, dtype))

    KaT = sb("KaT", [128, 1024], BF16)
    VaT = sb("VaT", [128, 1024], BF16)
    KbT = sb("KbT", [128, 4, SEQ], BF16)
    VbT = sb("VbT", [128, 4, SEQ], BF16)
    hT = sb("hT", [128, 8, T], F32)
    Ebf = sb("Ebf", [128, 8192], BF16)
    identf = sb("identf_sb", [128, 128], F32)
    identb = sb("identb", [128, 128], BF16)
    onesb = sb("onesb", [128, 128], BF16)
    vecs = sb("vecs_sb", [128, NV], F32)
    gsc = sb("gsc", [128, 64], F32)
    esink = sb("esink", [128, 4], F32)
    neghalf = sb("neghalf", [128, 4], F32)
    onesf = sb("onesf", [128, 128], F32)
    tsum_t = sb("tsum_t", [128, 4], F32)
    rstd_t = sb("rstd_t", [128, 4], F32)
    diag = sb("diag", [128, T], F32)
    fT = sb("fT", [128, 8, T], F32)
    xnT = sb("xnT", [128, 8, T], BF16)
    hid = sb("hid", [128, NHT, T], BF16)
    wring = sb("wring", [128, NSLOT, SLOT_ELEMS], BF16)
    QaP = sb("QaP", [128, 2, 4, T], BF16)
    QbD = sb("QbD", [128, 4, 2, T], BF16)
    tmpA = sb("tmpA", [128, T], F32)
    xin = sb("xin", [128, D], F32)
    xnb = sb("xnb", [128, 4, D], BF16)
    ssx = sb("ssx", [128, 4], F32)
    tsx = sb("tsx", [128, 4], F32)
    rsx = sb("rsx", [128, 4], F32)
    sq = sb("sq", [128, 2, T], BF16)
    rden = sb("rden", [128, T], F32)
    pT = tmpA[:, :].bitcast(BF16).rearrange("p (k t) -> p k t", k=2)
    pin = rden[:, :].bitcast(BF16).rearrange("p (s f) -> p s f", s=4)
    fT_flat = fT[:, :, :].rearrange("p k t -> p (k t)")
    xs = fT_flat.rearrange("p (s f) -> p s f", s=4)
    Vt = fT_flat.bitcast(BF16).rearrange("p (n f) -> p n f", f=128)
    NPB = 4
    PB = [ps(f"pb{i}", [128, 512], F32) for i in range(NPB)]
    PBH = [PB[i][:, :].bitcast(BF16) for i in range(NPB)]
    NUM = ps("num", [128, 512], F32)
    NUM2 = ps("num2", [128, 512], F32)
    DEN = ps("den", [128, 512], F32)
    SSQ = ps("ssq", [128, 512], F32)
    NUMS = [(NUM, "NUM"), (NUM2, "NUM2")]
    DENS = [(DEN, "DEN"), (SSQ, "SSQ")]

    state = {"pb": 0, "sq": 0, "praw": 0}
    dbg_keys = []

    def next_pb():
        i = state["pb"]
        state["pb"] = (i + 1) % NPB
        return i

    blocks = []
    def chunk_blocks():
        bl = []
        if debug_stage == "x":
            return bl
        bl += [("gu", 0, j) for j in range(NHT)]
        bl += [("dn", 0, mp, kh) for mp in range(4) for kh in range(2)]
        if debug_stage == "ffn1":
            return bl
        bl += [("in", cb) for cb in range(9)]
        bl += [("out", cb) for cb in range(4)]
        if debug_stage in ("attn", "mix"):
            return bl
        bl += [("gu", 1, j) for j in range(NHT)]
        bl += [("dn", 1, mp, kh) for mp in range(4) for kh in range(2)]
        if debug_stage == "ffn2":
            return bl
        bl += [("gate", cb) for cb in range(4)]
        bl += [("proj",)]
        return bl
    for c in range(nchunks):
        blocks += chunk_blocks()
    wstate = {"next_issue": 0, "next_use": 0}

    def slot_view(s, k, c):
        return wring[:, s, 0:k * c].rearrange("p (k c) -> p k c", k=k)

    def issue_block():
        bi = wstate["next_issue"]
        if bi >= len(blocks):
            return
        wstate["next_issue"] = bi + 1
        b = blocks[bi]
        s = bi % NSLOT
        kind = b[0]
        if kind == "gu":
            src = wgu_d[b[1]][:, 256 * b[2]:256 * b[2] + 256].rearrange("(k p) c -> p k c", p=128)
            dst = slot_view(s, 8, 256)
        elif kind == "dn":
            mp, kh = b[2], b[3]
            src = wd_d[b[1]][kh * 1408:(kh + 1) * 1408, 256 * mp:256 * mp + 256].rearrange("(k p) c -> p k c", p=128)
            dst = slot_view(s, 11, 256)
        elif kind == "in":
            src = win_d[:, 256 * b[1]:256 * b[1] + 256].rearrange("(k p) c -> p k c", p=128)
            dst = slot_view(s, 8, 256)
        elif kind == "out":
            src = wout_d[:, 256 * b[1]:256 * b[1] + 256].rearrange("(k p) c -> p k c", p=128)
            dst = slot_view(s, 8, 256)
        elif kind == "gate":
            src = wgate_d[:, 256 * b[1]:256 * b[1] + 256].rearrange("(k p) c -> p k c", p=128)
            dst = slot_view(s, 8, 256)
        else:
            src = wproj_d[:, :].rearrange("(k p) c -> p k c", p=128)
            dst = slot_view(s, 2, 1024)
        S.op("pool", lambda E, dst=dst, src=src: E.dma_start(out=dst, in_=src),
             writes=[("w", s)], dma=("w", s))

    def use_block(expect):
        bi = wstate["next_use"]
        wstate["next_use"] = bi + 1
        assert blocks[bi][0] == expect, (blocks[bi], expect)
        return bi % NSLOT

    ALLFT = [("fT", i) for i in range(8)]
    S.op("sp", lambda E: E.dma_start(out=vecs[:, :], in_=vecs_d[:, :]), writes=["vecs"], dma="c0")
    S.op("sp", lambda E: E.dma_start(out=identf[:, :], in_=identf_d[:, :]), writes=["identf"], dma="c1")
    S.op("sp", lambda E: E.dma_start(out=fT_flat, in_=ebias_d[:, 0:4096]), writes=ALLFT, dma="c2")
    for _ in range(NSLOT):
        issue_block()
    S.op("dve", lambda E: E.memset(KaT[:, :], 0.0), writes=["KaT"])
    S.op("dve", lambda E: E.memset(VaT[:, :], 0.0), writes=["VaT"])
    S.op("dve", lambda E: E.memset(QaP[:, :, :, :].rearrange("p a b t -> p (a b t)"), 0.0), writes=["QaP"])
    S.op("dve", lambda E: E.memset(QbD[:, :, :, :].rearrange("p a b t -> p (a b t)"), 0.0), writes=["QbD"])
    for pr in range(4):
        S.op("dve", lambda E, pr=pr: E.memset(KbT[:, pr, :], 0.0), writes=["KbT"])
        S.op("dve", lambda E, pr=pr: E.memset(VbT[:, pr, :], 0.0), writes=["VbT"])
    S.op("dve", lambda E: E.memset(onesb[:, :], 1.0), writes=["onesb"])
    S.op("dve", lambda E: E.memset(onesf[:, :], 1.0), writes=["onesf"])
    S.op("dve", lambda E: E.memset(neghalf[:, :], -0.5), writes=["neghalf"])
    S.op("dve", lambda E: E.tensor_copy(identb[:, :], identf[:, :]), reads=["identf"], writes=["identb"])
    for gi in range(8):
        scl = 16.0 if gi in (G_FFN1_POST, G_FFN2_POST) else 32.0
        S.op("dve", lambda E, gi=gi, scl=scl: E.tensor_scalar(gsc[:, 8 * gi:8 * gi + 8], vecs[:, NV_G + 8 * gi:NV_G + 8 * gi + 8],
                                                              scl, None, ALU.mult), reads=["vecs"], writes=["gsc"])
    S.op("act", lambda E: E.activation(esink[:, :], vecs[:, NV_SINK:NV_SINK + 4], AF.Exp), reads=["vecs"], writes=["esink"])
    S.op("act", lambda E: E.activation(Ebf[:, 0:4096], fT_flat, AF.Identity, scale=8.0), reads=ALLFT, writes=["Ebf"])
    S.op("sp", lambda E: E.dma_start(out=fT_flat, in_=ebias_d[:, 4096:8192]), writes=ALLFT, dma="c3")
    S.op("act", lambda E: E.activation(Ebf[:, 4096:8192], fT_flat, AF.Identity, scale=8.0), reads=ALLFT, writes=["Ebf"])

    CONSTS = ["identf", "identb", "onesb", "neghalf", "gsc", "esink", "Ebf", "vecs"]

    def norm_stats_from(src_fn, src_keys, bias_fn=None):
        for k in range(8):
            si = state["sq"]
            state["sq"] ^= 1
            if bias_fn is None:
                S.op("act", lambda E, k=k, si=si: E.activation(sq[:, si, :], src_fn(k), AF.Square),
                     reads=src_keys(k), writes=[("sq", si)])
            else:
                S.op("act", lambda E, k=k, si=si: E.activation(sq[:, si, :], src_fn(k), AF.Square, bias=bias_fn(k)),
                     reads=src_keys(k) + ["vecs"], writes=[("sq", si)])
            ssq_accum(k, si)
        finish_rstd(D * EPS)

    def finish_rstd(epsv):
        S.op("dve", lambda E: E.tensor_scalar(tsum_t[:, :], DEN[:, 0:4], float(epsv), None, ALU.add), reads=["DEN"], writes=["tsum_t"])
        S.op("pool", lambda E: E.tensor_tensor(rstd_t[:, :], tsum_t[:, :], neghalf[:, :], ALU.pow),
             reads=["tsum_t", "neghalf"], writes=["rstd_t"])
        S.op("dve", lambda E: E.tensor_tensor(diag[:, :].rearrange("p (t j) -> p t j", t=4),
                                              identf[:, :].unsqueeze(1).broadcast_to([128, 4, 128]),
                                              rstd_t[:, :].unsqueeze(2).broadcast_to([128, 4, 128]), ALU.mult),
             reads=["rstd_t", "identf"], writes=[("diag", tt) for tt in range(4)])
        for tt in range(4):
            S.op("pe", lambda E, tt=tt: E.matmul(SSQ[:, tt * 128:(tt + 1) * 128], onesf[:, :], diag[:, tt * 128:(tt + 1) * 128],
                                                 start=True, stop=True, skip_group_check=True),
                 reads=[("diag", tt), "onesf"], writes=["SSQ"])

    def pre_norm(gi):
        norm_stats_from(lambda k: hT[:, k, :], lambda k: [("hT", k)])
        for k in range(8):
            S.op("dve", lambda E, k=k: E.scalar_tensor_tensor(xnT[:, k, :], hT[:, k, :], gsc[:, 8 * gi + k:8 * gi + k + 1], SSQ[:, :],
                                                              ALU.mult, ALU.mult),
                 reads=[("hT", k), "SSQ", "gsc"], writes=[("xnT", k)])

    def post_update(gi):
        for m in range(8):
            S.op("dve", lambda E, m=m: E.tensor_tensor(fT[:, m, :], fT[:, m, :], SSQ[:, :], ALU.mult),
                 reads=[("fT", m), "SSQ"], writes=[("fT", m)])
        for m in range(8):
            S.op("dve", lambda E, m=m: E.scalar_tensor_tensor(hT[:, m, :], fT[:, m, :], gsc[:, 8 * gi + m:8 * gi + m + 1], hT[:, m, :],
                                                              ALU.mult, ALU.add),
                 reads=[("fT", m), ("hT", m), "gsc"], writes=[("hT", m)])

    def ssq_accum(m, si):
        for tt in range(4):
            S.op("pe", lambda E, m=m, si=si, tt=tt: E.matmul(DEN[:, tt:tt + 1], sq[:, si, tt * 128:(tt + 1) * 128], onesb[:, 0:1],
                                                             start=(m == 0 and tt == 0), stop=False, skip_group_check=True),
                 reads=[("sq", si), "onesb"], writes=["DEN"])

    def load_x(c):
        for tt in range(4):
            r0 = c * T + tt * 128
            S.op("sp", lambda E, tt=tt, r0=r0: E.dma_start(out=xs[:, tt, :], in_=x_d[r0:r0 + 128, :]),
                 writes=[("fT", 2 * tt), ("fT", 2 * tt + 1)], dma=("x", tt))
        for k in range(8):
            b = next_pb()
            for tt in range(4):
                S.op("pe", lambda E, k=k, tt=tt, b=b: E.transpose(PB[b][:, tt * 128:(tt + 1) * 128], xs[:, tt, k * 128:(k + 1) * 128], identf[:, :]),
                     reads=[("fT", 2 * tt), ("fT", 2 * tt + 1), "identf"], writes=[("pb", b)])
            eng = "dve" if k % 2 == 0 else "act"
            if eng == "dve":
                S.op("dve", lambda E, k=k, b=b: E.tensor_copy(hT[:, k, :], PB[b][:, :]), reads=[("pb", b)], writes=[("hT", k)])
            else:
                S.op("act", lambda E, k=k, b=b: E.activation(hT[:, k, :], PB[b][:, :], AF.Identity), reads=[("pb", b)], writes=[("hT", k)])

    def store_out(c):
        for tt in range(4):
            for half in range(2):
                b = next_pb()
                for kk in range(4):
                    k = half * 4 + kk
                    S.op("pe", lambda E, k=k, kk=kk, tt=tt, b=b: E.transpose(PB[b][:, kk * 128:(kk + 1) * 128], hT[:, k, tt * 128:(tt + 1) * 128], identf[:, :]),
                         reads=[("hT", k), "identf"], writes=[("pb", b)])
                if half == 0:
                    S.op("dve", lambda E, tt=tt, b=b: E.tensor_copy(xs[:, tt, 0:512], PB[b][:, :]), reads=[("pb", b)], writes=[("fT", 2 * tt)])
                else:
                    S.op("act", lambda E, tt=tt, b=b: E.activation(xs[:, tt, 512:1024], PB[b][:, :], AF.Identity), reads=[("pb", b)], writes=[("fT", 2 * tt + 1)])
            r0 = c * T + tt * 128
            S.op("sp", lambda E, tt=tt, r0=r0: E.dma_start(out=out_d[r0:r0 + 128, :], in_=xs[:, tt, :]),
                 reads=[("fT", 2 * tt), ("fT", 2 * tt + 1)], dma=("o", tt))

    def early_stats_tile(c, tt):
        r0 = c * T + tt * 128
        S.op("sp", lambda E, r0=r0: E.dma_start(out=xin[:, :], in_=x_d[r0:r0 + 128, :]), writes=["xin"], dma="xin")
        S.op("act", lambda E, tt=tt: E.activation(xnb[:, tt, :], xin[:, :], AF.Square, accum_out=ssx[:, tt:tt + 1]),
             reads=["xin"], writes=[("xnb", tt), ("ssx", tt)])
        S.op("dve", lambda E, tt=tt: E.tensor_scalar(tsx[:, tt:tt + 1], ssx[:, tt:tt + 1], float(D * EPS), None, ALU.add),
             reads=[("ssx", tt)], writes=[("tsx", tt)])
        S.op("pool", lambda E, tt=tt: E.tensor_tensor(rsx[:, tt:tt + 1], tsx[:, tt:tt + 1], neghalf[:, 0:1], ALU.pow),
             reads=[("tsx", tt), "neghalf"], writes=[("rsx", tt)])
        S.op("dve", lambda E, tt=tt: E.tensor_scalar(xnb[:, tt, :], xin[:, :], rsx[:, tt:tt + 1], None, ALU.mult),
             reads=["xin", ("rsx", tt)], writes=[("xnb", tt)])

    def early_transposes(gi):
        for tt in range(4):
            b = next_pb()
            for k in range(8):
                S.op("pe", lambda E, k=k, b=b, tt=tt: E.transpose(PBH[b][:, k * 128:(k + 1) * 128], xnb[:, tt, k * 128:(k + 1) * 128], identb[:, :]),
                     reads=[("xnb", tt), "identb"], writes=[("pb", b)])
            S.op("dve", lambda E, tt=tt, b=b: E.tensor_tensor(xnT[:, :, tt * 128:(tt + 1) * 128],
                                                             PBH[b][:, :].rearrange("p (k t) -> p k t", k=8),
                                                             gsc[:, 8 * gi:8 * gi + 8].unsqueeze(2).broadcast_to([128, 8, 128]), ALU.mult),
                 reads=[("pb", b), "gsc"], writes=[("xnT", k) for k in range(8)])

    def ffn(which, g_pre, g_post, skip_prenorm=False, hooks=None):
        hooks = hooks or {}
        if not skip_prenorm:
            pre_norm(g_pre)
        def gu_evac(j, bg, bu):
            S.op("act", lambda E, bg=bg: E.activation(tmpA[:, :], PB[bg][:, :], AF.Silu), reads=[("pb", bg)], writes=["tmpA"])
            S.op("dve", lambda E, j=j, bu=bu: E.tensor_tensor(hid[:, j, :], tmpA[:, :], PB[bu][:, :], ALU.mult),
                 reads=["tmpA", ("pb", bu)], writes=[("hid", j)])
        s0 = use_block("gu")
        s1 = use_block("gu")
        wv0 = slot_view(s0, 8, 256)
        wv1 = slot_view(s1, 8, 256)
        fb = [next_pb() for _ in range(4)]
        for k in range(8):
            for gi_, (wv_, s_, off_) in enumerate([(wv0, s0, 0), (wv0, s0, 128), (wv1, s1, 0), (wv1, s1, 128)]):
                S.op("pe", lambda E, k=k, wv_=wv_, off_=off_, b=fb[gi_]: E.matmul(PB[b][:, :], wv_[:, k, off_:off_ + 128], xnT[:, k, :], start=(k == 0), stop=(k == 7)),
                     reads=[("w", s_), ("xnT", k)], writes=[("pb", fb[gi_])])
        issue_block()
        issue_block()
        gu_evac(0, fb[0], fb[1])
        gu_evac(1, fb[2], fb[3])
        if 1 in hooks:
            hooks[1]()
        for j in range(2, NHT):
            s = use_block("gu")
            wv = slot_view(s, 8, 256)
            bg = next_pb()
            bu = next_pb()
            for k in range(8):
                S.op("pe", lambda E, k=k, wv=wv, bg=bg: E.matmul(PB[bg][:, :], wv[:, k, 0:128], xnT[:, k, :], start=(k == 0), stop=(k == 7)),
                     reads=[("w", s), ("xnT", k)], writes=[("pb", bg)])
            for k in range(8):
                S.op("pe", lambda E, k=k, wv=wv, bu=bu: E.matmul(PB[bu][:, :], wv[:, k, 128:256], xnT[:, k, :], start=(k == 0), stop=(k == 7)),
                     reads=[("w", s), ("xnT", k)], writes=[("pb", bu)])
            issue_block()
            gu_evac(j, bg, bu)
            if j in hooks:
                hooks[j]()
        pend = []
        for mp in range(4):
            b0 = next_pb()
            b1 = next_pb()
            bb = [b0, b1]
            for kh in range(2):
                s = use_block("dn")
                wv = slot_view(s, 11, 256)
                for mi in range(2):
                    for kk in range(11):
                        kg = kh * 11 + kk
                        S.op("pe", lambda E, wv=wv, mi=mi, kk=kk, kg=kg, b=bb[mi]: E.matmul(PB[b][:, :], wv[:, kk, mi * 128:(mi + 1) * 128], hid[:, kg, :],
                                                                                     start=(kg == 0), stop=(kg == NHT - 1)),
                             reads=[("w", s), ("hid", kg)], writes=[("pb", bb[mi])])
                issue_block()
            for p_ in pend:
                ssq_accum(*p_)
            pend = []
            for mi in range(2):
                m = 2 * mp + mi
                si = state["sq"]
                state["sq"] ^= 1
                S.op("dve", lambda E, m=m, b=bb[mi]: E.tensor_copy(fT[:, m, :], PB[b][:, :]), reads=[("pb", bb[mi])], writes=[("fT", m)])
                S.op("act", lambda E, si=si, m=m: E.activation(sq[:, si, :], fT[:, m, :], AF.Square), reads=[("fT", m)], writes=[("sq", si)])
                pend.append((m, si))
        for p_ in pend:
            ssq_accum(*p_)
        finish_rstd(D * EPS)
        post_update(g_post)

    def in_proj(c):
        pre_norm(G_ATT_PRE)

        def evac_tile(tile, b):
            bias = vecs[:, NV_BIN + tile:NV_BIN + tile + 1]
            if tile < 4 or 6 <= tile < 10:
                for hf in range(2):
                    rows = slice(64 * hf, 64 * hf + 64)
                    if tile < 4:
                        dst = QaP[rows, hf, tile, :]
                        wk = ["QaP"]
                    else:
                        dst = QbD[rows, tile - 6, hf, :]
                        wk = ["QbD"]
                    dstv = dst
                    srcv = PB[b][rows, :]
                    if tile % 2 == 0:
                        S.op("act", lambda E, dstv=dstv, srcv=srcv, tile=tile, rows=rows: E.activation(dstv, srcv, AF.Identity, bias=vecs[rows, NV_BIN + tile:NV_BIN + tile + 1]),
                             reads=[("pb", b), "vecs"], writes=wk)
                    else:
                        S.op("dve", lambda E, dstv=dstv, srcv=srcv, tile=tile, rows=rows: E.tensor_scalar(dstv, srcv, vecs[rows, NV_BIN + tile:NV_BIN + tile + 1], None, ALU.add),
                             reads=[("pb", b), "vecs"], writes=wk)
            else:
                if tile == 4:
                    dst, wk = KaT[:, (c % 2) * T:(c % 2) * T + T], ["KaT"]
                elif tile == 5:
                    dst, wk = VaT[:, (c % 2) * T:(c % 2) * T + T], ["VaT"]
                elif tile < 14:
                    dst, wk = KbT[:, tile - 10, c * T:(c + 1) * T], ["KbT"]
                else:
                    dst, wk = VbT[:, tile - 14, c * T:(c + 1) * T], ["VbT"]
                if tile % 2 == 0:
                    S.op("act", lambda E, dst=dst, b=b, bias=bias: E.activation(dst, PB[b][:, :], AF.Identity, bias=bias),
                         reads=[("pb", b), "vecs"], writes=wk)
                else:
                    S.op("dve", lambda E, dst=dst, b=b, bias=bias: E.tensor_scalar(dst, PB[b][:, :], bias, None, ALU.add),
                         reads=[("pb", b), "vecs"], writes=wk)

        s0 = use_block("in")
        s1 = use_block("in")
        wvs = [(slot_view(s0, 8, 256), s0), (slot_view(s1, 8, 256), s1)]
        fb = [next_pb() for _ in range(4)]
        for k in range(8):
            for t4 in range(4):
                wv_, s_ = wvs[t4 // 2]
                ti = t4 % 2
                S.op("pe", lambda E, k=k, wv_=wv_, ti=ti, b=fb[t4]: E.matmul(PB[b][:, :], wv_[:, k, ti * 128:(ti + 1) * 128], xnT[:, k, :], start=(k == 0), stop=(k == 7)),
                     reads=[("w", s_), ("xnT", k)], writes=[("pb", fb[t4])])
        issue_block()
        issue_block()
        for t4 in range(4):
            evac_tile(t4, fb[t4])
        for cb in range(2, 9):
            s = use_block("in")
            wv = slot_view(s, 8, 256)
            for ti in range(2):
                tile = 2 * cb + ti
                b = next_pb()
                for k in range(8):
                    S.op("pe", lambda E, k=k, wv=wv, ti=ti, b=b: E.matmul(PB[b][:, :], wv[:, k, ti * 128:(ti + 1) * 128], xnT[:, k, :], start=(k == 0), stop=(k == 7)),
                         reads=[("w", s), ("xnT", k)], writes=[("pb", b)])
                evac_tile(tile, b)
            issue_block()

    vt_state = {"n": None}

    def attention(c):
        def a_blk_off(bk):
            return 128 * (bk % 8)
        abks = [4 * c + j for j in range(-1, 4) if 4 * c + j >= 0]
        vta = {}
        nvt = 0
        S.alias_barrier([("fT", i) for i in range(8)], [("Vt", i) for i in range(64)])

        def flush_vt(batch, rkeys):
            n = len(batch)
            base = batch[0][0]
            ph = vt_state["n"]
            vt_state["n"] = None
            if False:
                S.op("act", lambda E, n=n, base=base, ph=ph: E.activation(Vt[:, base:base + n, :], PBH[ph][:, 0:n * 128].rearrange("p (n f) -> p n f", f=128), AF.Identity),
                     reads=[("pb", ph)], writes=[("Vt", i) for i, _ in batch])
            else:
                S.op("dve", lambda E, n=n, base=base, ph=ph: E.tensor_copy(Vt[:, base:base + n, :], PBH[ph][:, 0:n * 128].rearrange("p (n f) -> p n f", f=128)),
                     reads=[("pb", ph)], writes=[("Vt", i) for i, _ in batch])

        def vt_bank():
            if vt_state["n"] is None:
                vt_state["n"] = next_pb()
            return vt_state["n"]
        batch = []
        for bk in abks:
            vta[bk] = nvt
            off = a_blk_off(bk)
            slot = len(batch)
            ph = vt_bank()
            S.op("pe", lambda E, off=off, slot=slot, ph=ph: E.transpose(PBH[ph][:, slot * 128:(slot + 1) * 128], VaT[:, off:off + 128], identb[:, :]),
                 reads=["VaT", "identb"], writes=[("pb", ph)])
            batch.append((nvt, bk))
            nvt += 1
            if len(batch) == 4:
                flush_vt(batch, None)
                batch = []
        if batch:
            flush_vt(batch, None)
            batch = []
        pslot = {}
        ns = 0
        for j in range(4):
            for g in range(2):
                for blk in range(2):
                    bk = 4 * c + j - (1 - blk)
                    if bk < 0:
                        continue
                    b = next_pb()
                    off = a_blk_off(bk)
                    rv = QaP[:, g, :, 128 * j:128 * j + 128]
                    ov = PB[b][:, :]
                    ecol = E_A + (g * 2 + blk) * 512
                    S.op("pe", lambda E, off=off, rv=rv, ov=ov: E.matmul(ov, KaT[:, off:off + 128], rv, start=True, stop=False),
                         reads=["KaT", "QaP"], writes=[("pb", b)])
                    S.op("pe", lambda E, ov=ov, ecol=ecol: E.matmul(ov, identb[:, :], Ebf[:, ecol:ecol + 512], start=False, stop=True),
                         reads=["Ebf", "identb"], writes=[("pb", b)])
                    sl = ns
                    ns += 1
                    pslot[(j, g, blk)] = sl
                    S.op("act", lambda E, sl=sl, b=b: E.activation(hid[:, sl, :], PB[b][:, :], AF.Exp, scale=0.125),
                         reads=[("pb", b)], writes=[("hid", sl)])
        for hh in range(4):
            (NUMb, NK), (DENb, DK) = NUMS[hh % 2], DENS[hh % 2]
            first = [True, True]
            for j in range(4):
                for blk in range(2):
                    bk = 4 * c + j - (1 - blk)
                    if bk < 0:
                        continue
                    vt = vta[bk]
                    for hf in range(2):
                        rows = slice(64 * hf, 64 * hf + 64)
                        sl = pslot[(j, hf, blk)]
                        rhs = hid[:, sl, hh * 128:(hh + 1) * 128]
                        on = NUMb[rows, 128 * j:128 * j + 128]
                        od = DENb[rows, 128 * j:128 * j + 128]
                        st = first[hf]
                        first[hf] = False
                        S.op("pe", lambda E, on=on, vt=vt, hf=hf, rhs=rhs, st=st: E.matmul(on, Vt[:, vt, 64 * hf:64 * hf + 64], rhs, start=st, stop=False, skip_group_check=True),
                             reads=[("Vt", vt), ("hid", sl)], writes=[NK])
                        S.op("pe", lambda E, od=od, hf=hf, rhs=rhs, st=st: E.matmul(od, onesb[:, 0:64], rhs, start=st, stop=False, skip_group_check=True),
                             reads=[("hid", sl), "onesb"], writes=[DK])
            S.op("act", lambda E, hh=hh, DENb=DENb: E.activation(rden[:, :], DENb[:, :], AF.Ln, bias=esink[:, hh:hh + 1]), reads=[DK, "esink"], writes=["rden"])
            S.op("act", lambda E: E.activation(rden[:, :], rden[:, :], AF.Exp, scale=-1.0), reads=["rden"], writes=["rden"])
            S.op("dve", lambda E, hh=hh, NUMb=NUMb: E.tensor_tensor(xnT[:, hh, :], NUMb[:, :], rden[:, :], ALU.mult),
                 reads=[NK, "rden"], writes=[("xnT", hh)])
        cm = c % 4
        k1 = c // 4
        for pr in range(4):
            (NUMb, NK), (DENb, DK) = NUMS[pr % 2], DENS[pr % 2]
            vtb = {}
            nvt = 8
            batch = []
            def add_vt(key, ap):
                nonlocal nvt, batch
                slot = len(batch)
                vtb[key] = nvt
                ph = vt_bank()
                S.op("pe", lambda E, ap=ap, slot=slot, ph=ph: E.transpose(PBH[ph][:, slot * 128:(slot + 1) * 128], ap, identb[:, :]),
                     reads=["VbT", "identb"], writes=[("pb", ph)])
                batch.append((nvt, key))
                nvt += 1
                if len(batch) == 4:
                    flush_vt(batch, None)
                    batch = []
            for bk in abks:
                add_vt(("d1", bk), VbT[:, pr, 128 * bk:128 * bk + 128])
            for r4 in range(4):
                for cb_ in (c - 1, c):
                    if cb_ < 0:
                        continue
                    st_ = 512 * cb_ + r4
                    add_vt(("d4", r4, cb_), VbT[:, pr, st_:st_ + 509:4])
            for r in range(16):
                for kb in (k1 - 1, k1):
                    if kb < 0:
                        continue
                    st_ = 2048 * kb + r
                    add_vt(("d16", r, kb), VbT[:, pr, st_:st_ + 2033:16])
            if batch:
                flush_vt(batch, None)
                batch = []
            ps_ = {}
            ns = 0
            for j in range(4):
                b = next_pb()
                sl = ns
                ns += 1
                used = []
                for blk in range(2):
                    bk = 4 * c + j - (1 - blk)
                    if bk < 0:
                        continue
                    used.append(blk)
                    rv = QbD[:, pr, :, 128 * j:128 * j + 128]
                    ov = PB[b][:, blk * 256:(blk + 1) * 256]
                    ecb = E_1 + pr * 512 + blk * 256
                    S.op("pe", lambda E, bk=bk, rv=rv, ov=ov, pr=pr: E.matmul(ov, KbT[:, pr, 128 * bk:128 * bk + 128], rv, start=True, stop=False),
                         reads=["KbT", "QbD"], writes=[("pb", b)])
                    S.op("pe", lambda E, ov=ov, ecb=ecb: E.matmul(ov, identb[:, :], Ebf[:, ecb:ecb + 256], start=False, stop=True),
                         reads=["Ebf", "identb"], writes=[("pb", b)])
                    ps_[("d1", j, blk)] = (sl, blk * 256)
                c0 = used[0] * 256
                c1 = (used[-1] + 1) * 256
                ecol = E_1 + pr * 512
                S.op("act", lambda E, sl=sl, b=b, c0=c0, c1=c1: E.activation(hid[:, sl, c0:c1], PB[b][:, c0:c1], AF.Exp, scale=0.125),
                     reads=[("pb", b)], writes=[("hid", sl)])
            for r4 in range(4):
                b = next_pb()
                sl = ns
                ns += 1
                used = []
                for blk in range(2):
                    cb_ = c - (1 - blk)
                    if cb_ < 0:
                        continue
                    used.append(blk)
                    st_ = 512 * cb_ + r4
                    rv = QbD[:, pr, :, r4:512:4]
                    ov = PB[b][:, blk * 256:(blk + 1) * 256]
                    ecb = E_4 + pr * 512 + blk * 256
                    S.op("pe", lambda E, st_=st_, rv=rv, ov=ov, pr=pr: E.matmul(ov, KbT[:, pr, st_:st_ + 509:4], rv, start=True, stop=False),
                         reads=["KbT", "QbD"], writes=[("pb", b)])
                    S.op("pe", lambda E, ov=ov, ecb=ecb: E.matmul(ov, identb[:, :], Ebf[:, ecb:ecb + 256], start=False, stop=True),
                         reads=["Ebf", "identb"], writes=[("pb", b)])
                    ps_[("d4", r4, blk)] = (sl, blk * 256)
                c0 = used[0] * 256
                c1 = (used[-1] + 1) * 256
                ecol = E_4 + pr * 512
                S.op("act", lambda E, sl=sl, b=b, c0=c0, c1=c1: E.activation(hid[:, sl, c0:c1], PB[b][:, c0:c1], AF.Exp, scale=0.125),
                     reads=[("pb", b)], writes=[("hid", sl)])
            for blk in range(2):
                kb = k1 - (1 - blk)
                if kb < 0:
                    continue
                for rh in range(2):
                    b = next_pb()
                    sl = ns
                    ns += 1
                    for rr in range(8):
                        r = rh * 8 + rr
                        st_ = 2048 * kb + r
                        rv = QbD[:, pr, :, r:512:16]
                        ov = PB[b][:, rr * 64:(rr + 1) * 64]
                        ecol = E_16 + pr * 512 + (cm * 2 + blk) * 64
                        S.op("pe", lambda E, st_=st_, rv=rv, ov=ov, pr=pr: E.matmul(ov, KbT[:, pr, st_:st_ + 2033:16], rv, start=True, stop=False),
                             reads=["KbT", "QbD"], writes=[("pb", b)])
                        S.op("pe", lambda E, ov=ov, ecol=ecol: E.matmul(ov, identb[:, :], Ebf[:, ecol:ecol + 64], start=False, stop=True),
                             reads=["Ebf", "identb"], writes=[("pb", b)])
                        ps_[("d16", r, blk)] = (sl, rr * 64)
                    S.op("act", lambda E, sl=sl, b=b: E.activation(hid[:, sl, :], PB[b][:, :], AF.Exp, scale=0.125),
                         reads=[("pb", b)], writes=[("hid", sl)])
            first = [True, True]

            def pv(vt, sl, col, width, out_num, out_den, rhs_view):
                for eo in range(2):
                    rows = slice(64 * eo, 64 * eo + 64)
                    rhs = rhs_view(hid[:, sl, col + eo * width:col + (eo + 1) * width])
                    st = first[eo]
                    first[eo] = False
                    on = out_num(rows)
                    od = out_den(rows)
                    S.op("pe", lambda E, on=on, vt=vt, eo=eo, rhs=rhs, st=st: E.matmul(on, Vt[:, vt, 64 * eo:64 * eo + 64], rhs, start=st, stop=False, skip_group_check=True),
                         reads=[("Vt", vt), ("hid", sl)], writes=[NK])
                    S.op("pe", lambda E, od=od, rhs=rhs, st=st: E.matmul(od, onesb[:, 0:64], rhs, start=st, stop=False, skip_group_check=True),
                         reads=[("hid", sl), "onesb"], writes=[DK])
            for j in range(4):
                for blk in range(2):
                    bk = 4 * c + j - (1 - blk)
                    if bk < 0:
                        continue
                    sl, col = ps_[("d1", j, blk)]
                    pv(vtb[("d1", bk)], sl, col, 128,
                       lambda rows, j=j, NUMb=NUMb: NUMb[rows, 128 * j:128 * j + 128],
                       lambda rows, j=j, DENb=DENb: DENb[rows, 128 * j:128 * j + 128],
                       lambda ap: ap)
            for r4 in range(4):
                for blk in range(2):
                    cb_ = c - (1 - blk)
                    if cb_ < 0:
                        continue
                    sl, col = ps_[("d4", r4, blk)]
                    pv(vtb[("d4", r4, cb_)], sl, col, 128,
                       lambda rows, r4=r4, NUMb=NUMb: NUMb[rows, r4:512:4],
                       lambda rows, r4=r4, DENb=DENb: DENb[rows, r4:512:4],
                       lambda ap: ap)
            for r in range(16):
                for blk in range(2):
                    kb = k1 - (1 - blk)
                    if kb < 0:
                        continue
                    sl, col = ps_[("d16", r, blk)]
                    pv(vtb[("d16", r, kb)], sl, col, 32,
                       lambda rows, r=r, NUMb=NUMb: NUMb[rows, r:512:16],
                       lambda rows, r=r, DENb=DENb: DENb[rows, r:512:16],
                       lambda ap: ap)
            S.op("act", lambda E, DENb=DENb: E.activation(rden[:, :], DENb[:, :], AF.Ln), reads=[DK], writes=["rden"])
            S.op("act", lambda E: E.activation(rden[:, :], rden[:, :], AF.Exp, scale=-1.0), reads=["rden"], writes=["rden"])
            S.op("dve", lambda E, pr=pr, NUMb=NUMb: E.tensor_tensor(xnT[:, 4 + pr, :], NUMb[:, :], rden[:, :], ALU.mult),
                 reads=[NK, "rden"], writes=[("xnT", 4 + pr)])
        S.alias_barrier([("Vt", i) for i in range(64)], [("fT", i) for i in range(8)])

    def out_proj():
        pend = []
        for cb in range(4):
            s = use_block("out")
            wv = slot_view(s, 8, 256)
            for ti in range(2):
                m = 2 * cb + ti
                b = next_pb()
                for k in range(8):
                    S.op("pe", lambda E, k=k, wv=wv, ti=ti, b=b: E.matmul(PB[b][:, :], wv[:, k, ti * 128:(ti + 1) * 128], xnT[:, k, :], start=(k == 0), stop=(k == 7)),
                         reads=[("w", s), ("xnT", k)], writes=[("pb", b)])
                bias = vecs[:, NV_BOUT + m:NV_BOUT + m + 1]
                si = state["sq"]
                state["sq"] ^= 1
                S.op("dve", lambda E, m=m, b=b, bias=bias: E.tensor_scalar(fT[:, m, :], PB[b][:, :], bias, None, ALU.add),
                     reads=[("pb", b), "vecs"], writes=[("fT", m)])
                S.op("act", lambda E, si=si, m=m: E.activation(sq[:, si, :], fT[:, m, :], AF.Square),
                     reads=[("fT", m)], writes=[("sq", si)])
                for p_ in pend:
                    ssq_accum(*p_)
                pend = [(m, si)]
            issue_block()
        for p_ in pend:
            ssq_accum(*p_)
        finish_rstd(D * EPS)
        post_update(G_ATT_POST)

    def ple(c, after_gate=None):
        for tt in range(4):
            r0 = c * T + tt * 128
            S.op("pool", lambda E, tt=tt, r0=r0: E.dma_start(out=pin[:, tt, :], in_=p_d[r0:r0 + 128, :]),
                 writes=["rden"], dma=("p", tt))
        for k2 in range(2):
            bt = next_pb()
            for tt in range(4):
                S.op("pe", lambda E, k2=k2, tt=tt, bt=bt: E.transpose(PBH[bt][:, tt * 128:(tt + 1) * 128], pin[:, tt, k2 * 128:(k2 + 1) * 128], identb[:, :]),
                     reads=["rden", "identb"], writes=[("pb", bt)])
            S.op("act", lambda E, k2=k2, bt=bt: E.activation(pT[:, k2, :], PBH[bt][:, 0:512], AF.Identity), reads=[("pb", bt)], writes=["tmpA"])
        pre_norm(G_PLE_PRE)
        sp_ = None
        gate_slots = []
        pend = []
        for cb in range(4):
            s = use_block("gate")
            gate_slots.append(s)
            wv = slot_view(s, 8, 256)
            if cb == 0:
                sp_ = None
            for ti in range(2):
                m = 2 * cb + ti
                bgt = next_pb()
                for k in range(8):
                    S.op("pe", lambda E, k=k, wv=wv, ti=ti, b=bgt: E.matmul(PB[b][:, :], wv[:, k, ti * 128:(ti + 1) * 128], xnT[:, k, :], start=(k == 0), stop=(k == 7)),
                         reads=[("w", s), ("xnT", k)], writes=[("pb", bgt)])
                S.op("act", lambda E, m=m, b=bgt: E.activation(fT[:, m, :], PB[b][:, :], AF.Tanh, scale=0.5),
                     reads=[("pb", bgt)], writes=[("fT", m)])
            issue_block()
        if after_gate is not None:
            after_gate()
        s = use_block("proj")
        wv = slot_view(s, 2, 1024)
        for m in range(8):
            be = next_pb()
            for k2 in range(2):
                S.op("pe", lambda E, k2=k2, wv=wv, m=m, b=be: E.matmul(PB[b][:, :], wv[:, k2, m * 128:(m + 1) * 128], pT[:, k2, :], start=(k2 == 0), stop=(k2 == 1)),
                     reads=[("w", s), "tmpA"], writes=[("pb", be)])
            S.op("dve", lambda E, m=m, b=be: E.scalar_tensor_tensor(fT[:, m, :], fT[:, m, :], 1.0, PB[b][:, :], ALU.add, ALU.mult),
                 reads=[("fT", m), ("pb", be)], writes=[("fT", m)])
            si = state["sq"]
            state["sq"] ^= 1
            S.op("act", lambda E, m=m, si=si: E.activation(sq[:, si, :], fT[:, m, :], AF.Square), reads=[("fT", m)], writes=[("sq", si)])
            for p_ in pend:
                ssq_accum(*p_)
            pend = [(m, si)]
        issue_block()
        for p_ in pend:
            ssq_accum(*p_)

    def ple_tail():
        finish_rstd(4.0 * D * EPS)
        post_update(G_PLE_POST)

    full = debug_stage is None
    for c in range(nchunks):
        if full:
            if c == 0:
                for tt in range(4):
                    early_stats_tile(0, tt)
                early_transposes(G_FFN1_PRE)
                hooks = {1: (lambda: load_x(0))}
            else:
                hooks = {1: ple_tail, 4: (lambda c=c: store_out(c - 1)), 7: (lambda c=c: load_x(c))}
            ffn(0, G_FFN1_PRE, G_FFN1_POST, skip_prenorm=True, hooks=hooks)
        else:
            load_x(c)
            if debug_stage == "x":
                store_out(c)
                continue
            ffn(0, G_FFN1_PRE, G_FFN1_POST)
        if debug_stage != "ffn1":
            in_proj(c)
            attention(c)
            if debug_stage == "mix" and c == nchunks - 1:
                dbg = {}
                import os
                only = os.environ.get("DBG_DUMPS", "")
                def dump(name, ap, n, keys):
                    if only and name not in only.split(","):
                        return
                    d = nc.dram_tensor(name, [128, n], BF16, kind="ExternalOutput").ap()
                    S.op("sp", lambda E, d=d, ap=ap: E.dma_start(out=d[:, :], in_=ap), reads=keys, dma="dbg_" + name)
                    dbg_keys.extend(keys)
                dump("dbg_mix", xnT[:, :, :].rearrange("p k t -> p (k t)"), 4096, [("xnT", k) for k in range(8)])
                dump("dbg_qa", QaP[:, :, :, :].rearrange("p a b t -> p (a b t)"), 4096, ["QaP"])
                dump("dbg_qb", QbD[:, :, :, :].rearrange("p a b t -> p (a b t)"), 4096, ["QbD"])
                dump("dbg_ka", KaT[:, :], 1024, ["KaT"])
                dump("dbg_va", VaT[:, :], 1024, ["VaT"])
                for pr in range(4):
                    dump(f"dbg_kb{pr}", KbT[:, pr, 0:512], 512, ["KbT"])
                    dump(f"dbg_vb{pr}", VbT[:, pr, 0:512], 512, ["VbT"])
            out_proj()
            if debug_stage not in ("attn", "mix"):
                if full and c + 1 < nchunks:
                    h2 = {3 + 3 * tt: (lambda c=c, tt=tt: early_stats_tile(c + 1, tt)) for tt in range(4)}
                    ffn(1, G_FFN2_PRE, G_FFN2_POST, hooks=h2)
                    ple(c, after_gate=lambda: early_transposes(G_FFN1_PRE))
                    continue
                ffn(1, G_FFN2_PRE, G_FFN2_POST)
                if debug_stage != "ffn2":
                    ple(c)
                    ple_tail()
        store_out(c)
    S.op("sp", lambda E: None, reads=[("fT", i) for i in range(8)], writes=[("fT", i) for i in range(8)] + dbg_keys)

    S.finalize()
    sem_ids = list(Sched.ENGS) + S.dma_keys
    sems = {}
    for i, sid in enumerate(sem_ids):
        sems[sid] = es.enter_context(nc.semaphore(f"s{i}"))
    block = es.enter_context(nc.Block())
    S.emit(nc, block, sems)
    es.close()
    return nc


_CACHE = {}


def kernel(**inputs):
    shared = _prep_shared(inputs)
    x = np.asarray(inputs["x"], dtype=np.float32)
    p = np.asarray(inputs["p"], dtype=np.float32)
    in_maps = []
    for b in range(NB):
        m = dict(shared)
        m["x"] = np.ascontiguousarray(x[b])
        m["p"] = np.ascontiguousarray(p[0, b])
        in_maps.append(m)
    nc = build_program()
    res = run_bass_kernel_spmd(nc, in_maps, core_ids=list(range(NB)))
    out = np.stack([np.asarray(res.results[b]["out"], dtype=np.float32) for b in range(NB)], axis=0)
    return out
```

```python
import numpy as np
import concourse.bass as bass
import concourse.mybir as mybir
from concourse.bass_utils import run_bass_kernel_spmd

F32 = mybir.dt.float32
BF16 = mybir.dt.bfloat16
AF = mybir.ActivationFunctionType
ALU = mybir.AluOpType

D = 1024
SEQ = 4096
NB = 8
T = 512
NCH = SEQ // T
DFF = 2816
NHT = DFF // 128
DIN = 2304
PLE = 256
EPS = 1e-6
NEG = -30000.0
NSLOT = 4
SLOT_ELEMS = 2816

E_A, E_1, E_4, E_16 = 0, 2048, 4096, 6144
NV_G = 0
NV_BIN = 64
NV_BOUT = 82
NV_SINK = 90
NV = 94
G_FFN1_PRE, G_FFN1_POST, G_ATT_PRE, G_ATT_POST, G_FFN2_PRE, G_FFN2_POST, G_PLE_PRE, G_PLE_POST = range(8)

DEBUG_STAGE = None


class _Op:
    __slots__ = ("eng", "fn", "deps", "dma", "signal", "val")


class Sched:
    ENGS = ("pe", "act", "dve", "pool", "sp")
    STRICT = True

    def __init__(self):
        self.ops = {e: [] for e in self.ENGS}
        self.last_w = {}
        self.readers = {}
        self.dma_keys = []
        self.extra = {}

    def alias_barrier(self, from_keys, to_keys):
        deps = {}
        for k in from_keys:
            w = self.last_w.get(k)
            if w is not None:
                deps[id(w)] = w
            for r in self.readers.get(k, ()):
                deps[id(r)] = r
        for k in to_keys:
            self.extra.setdefault(k, {}).update(deps)

    def op(self, eng, fn, reads=(), writes=(), dma=None):
        o = _Op()
        o.eng, o.fn, o.dma, o.signal, o.val = eng, fn, dma, False, 0
        deps = {}
        for k in list(reads) + list(writes):
            ex = self.extra.pop(k, None)
            if ex:
                deps.update(ex)
        same = {}
        for k in reads:
            w = self.last_w.get(k)
            if w is not None:
                deps[id(w)] = w
                same[id(w)] = True
        for k in writes:
            w = self.last_w.get(k)
            if w is not None:
                deps[id(w)] = w
                same[id(w)] = True
            for r in self.readers.get(k, ()):
                deps[id(r)] = r
        o.deps = [d for d in deps.values()
                  if d is not o and (d.dma is not None or d.eng != eng or (self.STRICT and eng != "pe"))]
        for k in reads:
            self.readers.setdefault(k, []).append(o)
        for k in writes:
            self.last_w[k] = o
            self.readers[k] = []
        if dma is not None and dma not in self.dma_keys:
            self.dma_keys.append(dma)
        self.ops[eng].append(o)
        return o

    def finalize(self):
        for e in self.ENGS:
            for o in self.ops[e]:
                for d in o.deps:
                    d.signal = True
        dcount = {}
        for e in self.ENGS:
            c = 0
            for o in self.ops[e]:
                if o.dma is not None:
                    pass
                elif o.signal:
                    c += 1
                    o.val = c
        for e in self.ENGS:
            for o in self.ops[e]:
                if o.dma is not None:
                    dcount[o.dma] = dcount.get(o.dma, 0) + 16
                    o.val = dcount[o.dma]

    def emit(self, nc, block, sems):
        def make(eng_name):
            def body(E):
                waited = {}
                for o in self.ops[eng_name]:
                    need = {}
                    for d in o.deps:
                        sid = d.dma if d.dma is not None else d.eng
                        if d.val > need.get(sid, 0):
                            need[sid] = d.val
                    for sid, v in need.items():
                        if waited.get(sid, 0) >= v:
                            continue
                        E.wait_ge(sems[sid], v)
                        waited[sid] = v
                    ins = o.fn(E)
                    if ins is None:
                        continue
                    if o.dma is not None:
                        ins.then_inc(sems[o.dma], 16)
                    elif o.signal:
                        ins.then_inc(sems[o.eng], 1)
            return body
        block.tensor(make("pe"))
        block.scalar(make("act"))
        block.vector(make("dve"))
        block.gpsimd(make("pool"))
        block.sync(make("sp"))


def _t5_bucket(dist):
    n = np.maximum(dist, 0)
    nf = np.maximum(n, 1).astype(np.float64)
    large = 16 + (np.log(nf / 16.0) / np.log(2048.0 / 16.0) * 16.0).astype(np.int64)
    large = np.minimum(large, 31)
    return np.where(n < 16, n, large)


def _build_bias(rel_bias):
    rb = np.asarray(rel_bias, dtype=np.float32)
    E = np.full((128, 8192), NEG, dtype=np.float32)
    kk = np.arange(128)[:, None]
    i = np.arange(128)[None, :]
    u1 = i
    u4 = i
    for blk in range(2):
        off = 128 if blk == 0 else 0
        d1 = u1 - kk + off
        d4 = u4 - kk + off
        validA = (d1 >= 0) & (d1 <= 127)
        bA = _t5_bucket(d1)
        for g in range(2):
            for hh in range(4):
                h = hh + 4 * g
                col = E_A + (g * 2 + blk) * 512 + hh * 128
                E[:, col:col + 128] = np.where(validA, rb[bA, h], NEG)
        valid1 = (d1 >= 0) & (d1 <= 128)
        b1 = _t5_bucket(d1)
        valid4 = (d4 >= 0) & (d4 <= 128)
        b4 = _t5_bucket(d4 * 4)
        for pr in range(4):
            for eo in range(2):
                h = 8 + 2 * pr + eo
                col = E_1 + pr * 512 + blk * 256 + eo * 128
                E[:, col:col + 128] = np.where(valid1, rb[b1, h], NEG)
                col = E_4 + pr * 512 + blk * 256 + eo * 128
                E[:, col:col + 128] = np.where(valid4, rb[b4, h], NEG)
    i32 = np.arange(32)[None, :]
    for cm in range(4):
        for blk in range(2):
            off = 128 if blk == 0 else 0
            d16 = 32 * cm + i32 - kk + off
            valid = (d16 >= 0) & (d16 <= 128)
            b16 = _t5_bucket(d16 * 16)
            for pr in range(4):
                for eo in range(2):
                    h = 8 + 2 * pr + eo
                    col = E_16 + pr * 512 + (cm * 2 + blk) * 64 + eo * 32
                    E[:, col:col + 32] = np.where(valid, rb[b16, h], NEG)
    return E


_QA_HEAD_ORDER = [0, 4, 1, 5, 2, 6, 3, 7]


def _prep_shared(inp):
    f = lambda a: np.ascontiguousarray(np.asarray(a, dtype=np.float32))
    w_in = f(inp["w_in"][0])
    b_in = f(inp["b_in"][0])
    qcols = np.concatenate([np.arange(h * 64, h * 64 + 64) for h in _QA_HEAD_ORDER])
    perm = np.concatenate([qcols, np.arange(512, DIN)])
    w_in_p = np.ascontiguousarray(w_in[:, perm])
    b_in_p = b_in[perm]
    w_out = f(inp["w_out"][0])
    rperm = np.concatenate([qcols, np.arange(512, 1024)])
    w_out_p = np.ascontiguousarray(w_out[rperm, :])

    def gu_interleave(w):
        w = f(w)
        g = w[:, :DFF].reshape(D, NHT, 128)
        u = w[:, DFF:].reshape(D, NHT, 128)
        return np.ascontiguousarray(np.stack([g, u], axis=2).reshape(D, 2 * DFF))

    vecs = np.zeros((128, NV), dtype=np.float32)
    gains = [inp["ffn1_pre_g"], inp["ffn1_post_g"], inp["attn_pre_g"], inp["attn_post_g"],
             inp["ffn2_pre_g"], inp["ffn2_post_g"], inp["ple_pre_g"], inp["ple_post_g"]]
    for gi, g in enumerate(gains):
        vecs[:, NV_G + 8 * gi: NV_G + 8 * gi + 8] = f(g[0]).reshape(8, 128).T
    vecs[:, NV_BIN:NV_BIN + 18] = b_in_p.reshape(18, 128).T
    vecs[:, NV_BOUT:NV_BOUT + 8] = f(inp["b_out"][0]).reshape(8, 128).T
    sinks = f(inp["sinks"][0])
    for hh in range(4):
        vecs[0:64, NV_SINK + hh] = sinks[hh]
        vecs[64:128, NV_SINK + hh] = sinks[hh + 4]
    shared = {
        "w_gu1": gu_interleave(inp["ffn1_w_gu"][0]),
        "w_d1": f(inp["ffn1_w_down"][0]),
        "w_in": w_in_p,
        "w_out": w_out_p,
        "w_gu2": gu_interleave(inp["ffn2_w_gu"][0]),
        "w_d2": f(inp["ffn2_w_down"][0]),
        "w_gate": f(inp["w_ple_gate"][0]),
        "w_proj": f(inp["w_ple_proj"][0]),
        "vecs": vecs,
        "ebias": _build_bias(inp["rel_bias"]),
        "identf": np.eye(128, dtype=np.float32),
    }
    return shared


def build_program(nchunks=NCH, debug_stage=None):
    nc = bass.Bass("TRN2", target_bir_lowering=False)
    dt = lambda name, shape, kind: nc.dram_tensor(name, shape, F32, kind=kind).ap()
    x_d = dt("x", [SEQ, D], "ExternalInput")
    p_d = dt("p", [SEQ, PLE], "ExternalInput")
    wgu_d = [dt("w_gu1", [D, 2 * DFF], "ExternalInput"), dt("w_gu2", [D, 2 * DFF], "ExternalInput")]
    wd_d = [dt("w_d1", [DFF, D], "ExternalInput"), dt("w_d2", [DFF, D], "ExternalInput")]
    win_d = dt("w_in", [D, DIN], "ExternalInput")
    wout_d = dt("w_out", [D, D], "ExternalInput")
    wgate_d = dt("w_gate", [D, D], "ExternalInput")
    wproj_d = dt("w_proj", [PLE, D], "ExternalInput")
    vecs_d = dt("vecs", [128, NV], "ExternalInput")
    ebias_d = dt("ebias", [128, 8192], "ExternalInput")
    identf_d = dt("identf", [128, 128], "ExternalInput")
    out_d = dt("out", [SEQ, D], "ExternalOutput")

    S = Sched()
    from contextlib import ExitStack
    es = ExitStack()
    sb = lambda name, shape, dtype: es.enter_context(nc.sbuf_tensor(name, shape, dtype))
    ps = lambda name, shape, dtype: es.enter_context(nc.psum_tensor(name, shape, dtype))

    KaT = sb("KaT", [128, 1024], BF16)
    VaT = sb("VaT", [128, 1024], BF16)
    KbT = sb("KbT", [128, 4, SEQ], BF16)
    VbT = sb("VbT", [128, 4, SEQ], BF16)
    hT = sb("hT", [128, 8, T], F32)
    Ebf = sb("Ebf", [128, 8192], BF16)
    identf = sb("identf_sb", [128, 128], F32)
    identb = sb("identb", [128, 128], BF16)
    onesb = sb("onesb", [128, 128], BF16)
    vecs = sb("vecs_sb", [128, NV], F32)
    gsc = sb("gsc", [128, 64], F32)
    esink = sb("esink", [128, 4], F32)
    neghalf = sb("neghalf", [128, 4], F32)
    onesf = sb("onesf", [128, 128], F32)
    tsum_t = sb("tsum_t", [128, 4], F32)
    rstd_t = sb("rstd_t", [128, 4], F32)
    diag = sb("diag", [128, T], F32)
    fT = sb("fT", [128, 8, T], F32)
    xnT = sb("xnT", [128, 8, T], BF16)
    hid = sb("hid", [128, NHT, T], BF16)
    wring = sb("wring", [128, NSLOT, SLOT_ELEMS], BF16)
    QaP = sb("QaP", [128, 2, 4, T], BF16)
    QbD = sb("QbD", [128, 4, 2, T], BF16)
    tmpA = sb("tmpA", [128, T], F32)
    xin = sb("xin", [128, D], F32)
    xnb = sb("xnb", [128, 4, D], BF16)
    ssx = sb("ssx", [128, 4], F32)
    tsx = sb("tsx", [128, 4], F32)
    rsx = sb("rsx", [128, 4], F32)
    sq = sb("sq", [128, 2, T], BF16)
    rden = sb("rden", [128, T], F32)
    pT = tmpA[:, :].bitcast(BF16).rearrange("p (k t) -> p k t", k=2)
    pin = rden[:, :].bitcast(BF16).rearrange("p (s f) -> p s f", s=4)
    fT_flat = fT[:, :, :].rearrange("p k t -> p (k t)")
    xs = fT_flat.rearrange("p (s f) -> p s f", s=4)
    Vt = fT_flat.bitcast(BF16).rearrange("p (n f) -> p n f", f=128)
    NPB = 4
    PB = [ps(f"pb{i}", [128, 512], F32) for i in range(NPB)]
    PBH = [PB[i][:, :].bitcast(BF16) for i in range(NPB)]
    NUM = ps("num", [128, 512], F32)
    NUM2 = ps("num2", [128, 512], F32)
    DEN = ps("den", [128, 512], F32)
    SSQ = ps("ssq", [128, 512], F32)
    NUMS = [(NUM, "NUM"), (NUM2, "NUM2")]
    DENS = [(DEN, "DEN"), (SSQ, "SSQ")]

    state = {"pb": 0, "sq": 0, "praw": 0}
    dbg_keys = []

    def next_pb():
        i = state["pb"]
        state["pb"] = (i + 1) % NPB
        return i

    blocks = []
    def chunk_blocks():
        bl = []
        if debug_stage == "x":
            return bl
        bl += [("gu", 0, j) for j in range(NHT)]
        bl += [("dn", 0, mp, kh) for mp in range(4) for kh in range(2)]
        if debug_stage == "ffn1":
            return bl
        bl += [("in", cb) for cb in range(9)]
        bl += [("out", cb) for cb in range(4)]
        if debug_stage in ("attn", "mix"):
            return bl
        bl += [("gu", 1, j) for j in range(NHT)]
        bl += [("dn", 1, mp, kh) for mp in range(4) for kh in range(2)]
        if debug_stage == "ffn2":
            return bl
        bl += [("gate", cb) for cb in range(4)]
        bl += [("proj",)]
        return bl
    for c in range(nchunks):
        blocks += chunk_blocks()
    wstate = {"next_issue": 0, "next_use": 0}

    def slot_view(s, k, c):
        return wring[:, s, 0:k * c].rearrange("p (k c) -> p k c", k=k)

    def issue_block():
        bi = wstate["next_issue"]
        if bi >= len(blocks):
            return
        wstate["next_issue"] = bi + 1
        b = blocks[bi]
        s = bi % NSLOT
        kind = b[0]
        if kind == "gu":
            src = wgu_d[b[1]][:, 256 * b[2]:256 * b[2] + 256].rearrange("(k p) c -> p k c", p=128)
            dst = slot_view(s, 8, 256)
        elif kind == "dn":
            mp, kh = b[2], b[3]
            src = wd_d[b[1]][kh * 1408:(kh + 1) * 1408, 256 * mp:256 * mp + 256].rearrange("(k p) c -> p k c", p=128)
            dst = slot_view(s, 11, 256)
        elif kind == "in":
            src = win_d[:, 256 * b[1]:256 * b[1] + 256].rearrange("(k p) c -> p k c", p=128)
            dst = slot_view(s, 8, 256)
        elif kind == "out":
            src = wout_d[:, 256 * b[1]:256 * b[1] + 256].rearrange("(k p) c -> p k c", p=128)
            dst = slot_view(s, 8, 256)
        elif kind == "gate":
            src = wgate_d[:, 256 * b[1]:256 * b[1] + 256].rearrange("(k p) c -> p k c", p=128)
            dst = slot_view(s, 8, 256)
        else:
            src = wproj_d[:, :].rearrange("(k p) c -> p k c", p=128)
            dst = slot_view(s, 2, 1024)
        S.op("pool", lambda E, dst=dst, src=src: E.dma_start(out=dst, in_=src),
             writes=[("w", s)], dma=("w", s))

    def use_block(expect):
        bi = wstate["next_use"]
        wstate["next_use"] = bi + 1
        assert blocks[bi][0] == expect, (blocks[bi], expect)
        return bi % NSLOT

    ALLFT = [("fT", i) for i in range(8)]
    S.op("sp", lambda E: E.dma_start(out=vecs[:, :], in_=vecs_d[:, :]), writes=["vecs"], dma="c0")
    S.op("sp", lambda E: E.dma_start(out=identf[:, :], in_=identf_d[:, :]), writes=["identf"], dma="c1")
    S.op("sp", lambda E: E.dma_start(out=fT_flat, in_=ebias_d[:, 0:4096]), writes=ALLFT, dma="c2")
    for _ in range(NSLOT):
        issue_block()
    S.op("dve", lambda E: E.memset(KaT[:, :], 0.0), writes=["KaT"])
    S.op("dve", lambda E: E.memset(VaT[:, :], 0.0), writes=["VaT"])
    S.op("dve", lambda E: E.memset(QaP[:, :, :, :].rearrange("p a b t -> p (a b t)"), 0.0), writes=["QaP"])
    S.op("dve", lambda E: E.memset(QbD[:, :, :, :].rearrange("p a b t -> p (a b t)"), 0.0), writes=["QbD"])
    for pr in range(4):
        S.op("dve", lambda E, pr=pr: E.memset(KbT[:, pr, :], 0.0), writes=["KbT"])
        S.op("dve", lambda E, pr=pr: E.memset(VbT[:, pr, :], 0.0), writes=["VbT"])
    S.op("dve", lambda E: E.memset(onesb[:, :], 1.0), writes=["onesb"])
    S.op("dve", lambda E: E.memset(onesf[:, :], 1.0), writes=["onesf"])
    S.op("dve", lambda E: E.memset(neghalf[:, :], -0.5), writes=["neghalf"])
    S.op("dve", lambda E: E.tensor_copy(identb[:, :], identf[:, :]), reads=["identf"], writes=["identb"])
    for gi in range(8):
        scl = 16.0 if gi in (G_FFN1_POST, G_FFN2_POST) else 32.0
        S.op("dve", lambda E, gi=gi, scl=scl: E.tensor_scalar(gsc[:, 8 * gi:8 * gi + 8], vecs[:, NV_G + 8 * gi:NV_G + 8 * gi + 8],
                                                              scl, None, ALU.mult), reads=["vecs"], writes=["gsc"])
    S.op("act", lambda E: E.activation(esink[:, :], vecs[:, NV_SINK:NV_SINK + 4], AF.Exp), reads=["vecs"], writes=["esink"])
    S.op("act", lambda E: E.activation(Ebf[:, 0:4096], fT_flat, AF.Identity, scale=8.0), reads=ALLFT, writes=["Ebf"])
    S.op("sp", lambda E: E.dma_start(out=fT_flat, in_=ebias_d[:, 4096:8192]), writes=ALLFT, dma="c3")
    S.op("act", lambda E: E.activation(Ebf[:, 4096:8192], fT_flat, AF.Identity, scale=8.0), reads=ALLFT, writes=["Ebf"])

    CONSTS = ["identf", "identb", "onesb", "neghalf", "gsc", "esink", "Ebf", "vecs"]

    def norm_stats_from(src_fn, src_keys, bias_fn=None):
        for k in range(8):
            si = state["sq"]
            state["sq"] ^= 1
            if bias_fn is None:
                S.op("act", lambda E, k=k, si=si: E.activation(sq[:, si, :], src_fn(k), AF.Square),
                     reads=src_keys(k), writes=[("sq", si)])
            else:
                S.op("act", lambda E, k=k, si=si: E.activation(sq[:, si, :], src_fn(k), AF.Square, bias=bias_fn(k)),
                     reads=src_keys(k) + ["vecs"], writes=[("sq", si)])
            ssq_accum(k, si)
        finish_rstd(D * EPS)

    def finish_rstd(epsv, bcast=True):
        S.op("dve", lambda E: E.tensor_scalar(tsum_t[:, :], DEN[:, 0:4], float(epsv), None, ALU.add), reads=["DEN"], writes=["tsum_t"])
        S.op("pool", lambda E: E.tensor_tensor(rstd_t[:, :], tsum_t[:, :], neghalf[:, :], ALU.pow),
             reads=["tsum_t", "neghalf"], writes=["rstd_t"])
        S.op("dve", lambda E: E.tensor_tensor(diag[:, :].rearrange("p (t j) -> p t j", t=4),
                                              identf[:, :].unsqueeze(1).broadcast_to([128, 4, 128]),
                                              rstd_t[:, :].unsqueeze(2).broadcast_to([128, 4, 128]), ALU.mult),
             reads=["rstd_t", "identf"], writes=[("diag", tt) for tt in range(4)])
        if bcast:
            rstd_bcast()

    def rstd_bcast():
        for tt in range(4):
            S.op("pe", lambda E, tt=tt: E.matmul(SSQ[:, tt * 128:(tt + 1) * 128], onesf[:, :], diag[:, tt * 128:(tt + 1) * 128],
                                                 start=True, stop=True, skip_group_check=True),
                 reads=[("diag", tt), "onesf"], writes=["SSQ"])

    def pre_norm(gi):
        norm_stats_from(lambda k: hT[:, k, :], lambda k: [("hT", k)])
        for k in range(8):
            S.op("dve", lambda E, k=k: E.scalar_tensor_tensor(xnT[:, k, :], hT[:, k, :], gsc[:, 8 * gi + k:8 * gi + k + 1], SSQ[:, :],
                                                              ALU.mult, ALU.mult),
                 reads=[("hT", k), "SSQ", "gsc"], writes=[("xnT", k)])

    def post_update(gi):
        for m in range(8):
            S.op("dve", lambda E, m=m: E.tensor_tensor(fT[:, m, :], fT[:, m, :], SSQ[:, :], ALU.mult),
                 reads=[("fT", m), "SSQ"], writes=[("fT", m)])
        for m in range(8):
            S.op("dve", lambda E, m=m: E.scalar_tensor_tensor(hT[:, m, :], fT[:, m, :], gsc[:, 8 * gi + m:8 * gi + m + 1], hT[:, m, :],
                                                              ALU.mult, ALU.add),
                 reads=[("fT", m), ("hT", m), "gsc"], writes=[("hT", m)])

    def ssq_accum(m, si):
        for tt in range(4):
            S.op("pe", lambda E, m=m, si=si, tt=tt: E.matmul(DEN[:, tt:tt + 1], sq[:, si, tt * 128:(tt + 1) * 128], onesb[:, 0:1],
                                                             start=(m == 0 and tt == 0), stop=False, skip_group_check=True),
                 reads=[("sq", si), "onesb"], writes=["DEN"])

    def load_x(c):
        for tt in range(4):
            r0 = c * T + tt * 128
            S.op("sp", lambda E, tt=tt, r0=r0: E.dma_start(out=xs[:, tt, :], in_=x_d[r0:r0 + 128, :]),
                 writes=[("fT", 2 * tt), ("fT", 2 * tt + 1)], dma=("x", tt))
        for k in range(8):
            b = next_pb()
            for tt in range(4):
                S.op("pe", lambda E, k=k, tt=tt, b=b: E.transpose(PB[b][:, tt * 128:(tt + 1) * 128], xs[:, tt, k * 128:(k + 1) * 128], identf[:, :]),
                     reads=[("fT", 2 * tt), ("fT", 2 * tt + 1), "identf"], writes=[("pb", b)])
            eng = "dve" if k % 2 == 0 else "act"
            if eng == "dve":
                S.op("dve", lambda E, k=k, b=b: E.tensor_copy(hT[:, k, :], PB[b][:, :]), reads=[("pb", b)], writes=[("hT", k)])
            else:
                S.op("act", lambda E, k=k, b=b: E.activation(hT[:, k, :], PB[b][:, :], AF.Identity), reads=[("pb", b)], writes=[("hT", k)])

    def store_out(c):
        for tt in range(4):
            for half in range(2):
                b = next_pb()
                for kk in range(4):
                    k = half * 4 + kk
                    S.op("pe", lambda E, k=k, kk=kk, tt=tt, b=b: E.transpose(PB[b][:, kk * 128:(kk + 1) * 128], hT[:, k, tt * 128:(tt + 1) * 128], identf[:, :]),
                         reads=[("hT", k), "identf"], writes=[("pb", b)])
                if half == 0:
                    S.op("dve", lambda E, tt=tt, b=b: E.tensor_copy(xs[:, tt, 0:512], PB[b][:, :]), reads=[("pb", b)], writes=[("fT", 2 * tt)])
                else:
                    S.op("act", lambda E, tt=tt, b=b: E.activation(xs[:, tt, 512:1024], PB[b][:, :], AF.Identity), reads=[("pb", b)], writes=[("fT", 2 * tt + 1)])
            r0 = c * T + tt * 128
            S.op("sp", lambda E, tt=tt, r0=r0: E.dma_start(out=out_d[r0:r0 + 128, :], in_=xs[:, tt, :]),
                 reads=[("fT", 2 * tt), ("fT", 2 * tt + 1)], dma=("o", tt))

    def early_stats_tile(c, tt):
        r0 = c * T + tt * 128
        S.op("sp", lambda E, r0=r0: E.dma_start(out=xin[:, :], in_=x_d[r0:r0 + 128, :]), writes=["xin"], dma="xin")
        S.op("act", lambda E, tt=tt: E.activation(xnb[:, tt, :], xin[:, :], AF.Square, accum_out=ssx[:, tt:tt + 1]),
             reads=["xin"], writes=[("xnb", tt), ("ssx", tt)])
        S.op("dve", lambda E, tt=tt: E.tensor_scalar(tsx[:, tt:tt + 1], ssx[:, tt:tt + 1], float(D * EPS), None, ALU.add),
             reads=[("ssx", tt)], writes=[("tsx", tt)])
        S.op("pool", lambda E, tt=tt: E.tensor_tensor(rsx[:, tt:tt + 1], tsx[:, tt:tt + 1], neghalf[:, 0:1], ALU.pow),
             reads=[("tsx", tt), "neghalf"], writes=[("rsx", tt)])
        S.op("dve", lambda E, tt=tt: E.tensor_scalar(xnb[:, tt, :], xin[:, :], rsx[:, tt:tt + 1], None, ALU.mult),
             reads=["xin", ("rsx", tt)], writes=[("xnb", tt)])

    def early_transposes(gi):
        for tt in range(4):
            b = next_pb()
            for k in range(8):
                S.op("pe", lambda E, k=k, b=b, tt=tt: E.transpose(PBH[b][:, k * 128:(k + 1) * 128], xnb[:, tt, k * 128:(k + 1) * 128], identb[:, :]),
                     reads=[("xnb", tt), "identb"], writes=[("pb", b)])
            S.op("dve", lambda E, tt=tt, b=b: E.tensor_tensor(xnT[:, :, tt * 128:(tt + 1) * 128],
                                                             PBH[b][:, :].rearrange("p (k t) -> p k t", k=8),
                                                             gsc[:, 8 * gi:8 * gi + 8].unsqueeze(2).broadcast_to([128, 8, 128]), ALU.mult),
                 reads=[("pb", b), "gsc"], writes=[("xnT", k) for k in range(8)])

    def ffn(which, g_pre, g_post, skip_prenorm=False, hooks=None):
        hooks = hooks or {}
        if not skip_prenorm:
            pre_norm(g_pre)
        def gu_evac(j, bg, bu):
            S.op("act", lambda E, bg=bg: E.activation(tmpA[:, :], PB[bg][:, :], AF.Silu), reads=[("pb", bg)], writes=["tmpA"])
            S.op("dve", lambda E, j=j, bu=bu: E.tensor_tensor(hid[:, j, :], tmpA[:, :], PB[bu][:, :], ALU.mult),
                 reads=["tmpA", ("pb", bu)], writes=[("hid", j)])
        s0 = use_block("gu")
        s1 = use_block("gu")
        wv0 = slot_view(s0, 8, 256)
        wv1 = slot_view(s1, 8, 256)
        fb = [next_pb() for _ in range(4)]
        for k in range(8):
            for gi_, (wv_, s_, off_) in enumerate([(wv0, s0, 0), (wv0, s0, 128), (wv1, s1, 0), (wv1, s1, 128)]):
                S.op("pe", lambda E, k=k, wv_=wv_, off_=off_, b=fb[gi_]: E.matmul(PB[b][:, :], wv_[:, k, off_:off_ + 128], xnT[:, k, :], start=(k == 0), stop=(k == 7)),
                     reads=[("w", s_), ("xnT", k)], writes=[("pb", fb[gi_])])
        issue_block()
        issue_block()
        gu_evac(0, fb[0], fb[1])
        gu_evac(1, fb[2], fb[3])
        if 1 in hooks:
            hooks[1]()
        for j in range(2, NHT):
            s = use_block("gu")
            wv = slot_view(s, 8, 256)
            bg = next_pb()
            bu = next_pb()
            for k in range(8):
                S.op("pe", lambda E, k=k, wv=wv, bg=bg: E.matmul(PB[bg][:, :], wv[:, k, 0:128], xnT[:, k, :], start=(k == 0), stop=(k == 7)),
                     reads=[("w", s), ("xnT", k)], writes=[("pb", bg)])
            for k in range(8):
                S.op("pe", lambda E, k=k, wv=wv, bu=bu: E.matmul(PB[bu][:, :], wv[:, k, 128:256], xnT[:, k, :], start=(k == 0), stop=(k == 7)),
                     reads=[("w", s), ("xnT", k)], writes=[("pb", bu)])
            issue_block()
            gu_evac(j, bg, bu)
            if j in hooks:
                hooks[j]()
        pend = []
        for mp in range(4):
            b0 = next_pb()
            b1 = next_pb()
            bb = [b0, b1]
            for kh in range(2):
                s = use_block("dn")
                wv = slot_view(s, 11, 256)
                for mi in range(2):
                    for kk in range(11):
                        kg = kh * 11 + kk
                        S.op("pe", lambda E, wv=wv, mi=mi, kk=kk, kg=kg, b=bb[mi]: E.matmul(PB[b][:, :], wv[:, kk, mi * 128:(mi + 1) * 128], hid[:, kg, :],
                                                                                     start=(kg == 0), stop=(kg == NHT - 1)),
                             reads=[("w", s), ("hid", kg)], writes=[("pb", bb[mi])])
                issue_block()
            for p_ in pend:
                ssq_accum(*p_)
            pend = []
            for mi in range(2):
                m = 2 * mp + mi
                si = state["sq"]
                state["sq"] ^= 1
                S.op("dve", lambda E, m=m, b=bb[mi]: E.tensor_copy(fT[:, m, :], PB[b][:, :]), reads=[("pb", bb[mi])], writes=[("fT", m)])
                S.op("act", lambda E, si=si, m=m: E.activation(sq[:, si, :], fT[:, m, :], AF.Square), reads=[("fT", m)], writes=[("sq", si)])
                pend.append((m, si))
        for p_ in pend:
            ssq_accum(*p_)
        finish_rstd(D * EPS)
        post_update(g_post)

    def in_proj(c):
        pre_norm(G_ATT_PRE)

        def evac_tile(tile, b):
            bias = vecs[:, NV_BIN + tile:NV_BIN + tile + 1]
            if tile < 4 or 6 <= tile < 10:
                for hf in range(2):
                    rows = slice(64 * hf, 64 * hf + 64)
                    if tile < 4:
                        dst = QaP[rows, hf, tile, :]
                        wk = ["QaP"]
                    else:
                        dst = QbD[rows, tile - 6, hf, :]
                        wk = ["QbD"]
                    dstv = dst
                    srcv = PB[b][rows, :]
                    if tile % 2 == 0:
                        S.op("act", lambda E, dstv=dstv, srcv=srcv, tile=tile, rows=rows: E.activation(dstv, srcv, AF.Identity, bias=vecs[rows, NV_BIN + tile:NV_BIN + tile + 1]),
                             reads=[("pb", b), "vecs"], writes=wk)
                    else:
                        S.op("dve", lambda E, dstv=dstv, srcv=srcv, tile=tile, rows=rows: E.tensor_scalar(dstv, srcv, vecs[rows, NV_BIN + tile:NV_BIN + tile + 1], None, ALU.add),
                             reads=[("pb", b), "vecs"], writes=wk)
            else:
                if tile == 4:
                    dst, wk = KaT[:, (c % 2) * T:(c % 2) * T + T], ["KaT"]
                elif tile == 5:
                    dst, wk = VaT[:, (c % 2) * T:(c % 2) * T + T], ["VaT"]
                elif tile < 14:
                    dst, wk = KbT[:, tile - 10, c * T:(c + 1) * T], ["KbT"]
                else:
                    dst, wk = VbT[:, tile - 14, c * T:(c + 1) * T], ["VbT"]
                if tile % 2 == 0:
                    S.op("act", lambda E, dst=dst, b=b, bias=bias: E.activation(dst, PB[b][:, :], AF.Identity, bias=bias),
                         reads=[("pb", b), "vecs"], writes=wk)
                else:
                    S.op("dve", lambda E, dst=dst, b=b, bias=bias: E.tensor_scalar(dst, PB[b][:, :], bias, None, ALU.add),
                         reads=[("pb", b), "vecs"], writes=wk)

        s0 = use_block("in")
        s1 = use_block("in")
        wvs = [(slot_view(s0, 8, 256), s0), (slot_view(s1, 8, 256), s1)]
        fb = [next_pb() for _ in range(4)]
        for k in range(8):
            for t4 in range(4):
                wv_, s_ = wvs[t4 // 2]
                ti = t4 % 2
                S.op("pe", lambda E, k=k, wv_=wv_, ti=ti, b=fb[t4]: E.matmul(PB[b][:, :], wv_[:, k, ti * 128:(ti + 1) * 128], xnT[:, k, :], start=(k == 0), stop=(k == 7)),
                     reads=[("w", s_), ("xnT", k)], writes=[("pb", fb[t4])])
        issue_block()
        issue_block()
        for t4 in range(4):
            evac_tile(t4, fb[t4])
        for cb in range(2, 9):
            s = use_block("in")
            wv = slot_view(s, 8, 256)
            for ti in range(2):
                tile = 2 * cb + ti
                b = next_pb()
                for k in range(8):
                    S.op("pe", lambda E, k=k, wv=wv, ti=ti, b=b: E.matmul(PB[b][:, :], wv[:, k, ti * 128:(ti + 1) * 128], xnT[:, k, :], start=(k == 0), stop=(k == 7)),
                         reads=[("w", s), ("xnT", k)], writes=[("pb", b)])
                evac_tile(tile, b)
            issue_block()

    vt_state = {"n": None}

    def attention(c):
        def a_blk_off(bk):
            return 128 * (bk % 8)
        abks = [4 * c + j for j in range(-1, 4) if 4 * c + j >= 0]
        vta = {}
        nvt = 0
        S.alias_barrier([("fT", i) for i in range(8)], [("Vt", i) for i in range(64)])

        def flush_vt(batch, rkeys):
            n = len(batch)
            base = batch[0][0]
            ph = vt_state["n"]
            vt_state["n"] = None
            if False:
                S.op("act", lambda E, n=n, base=base, ph=ph: E.activation(Vt[:, base:base + n, :], PBH[ph][:, 0:n * 128].rearrange("p (n f) -> p n f", f=128), AF.Identity),
                     reads=[("pb", ph)], writes=[("Vt", i) for i, _ in batch])
            else:
                S.op("dve", lambda E, n=n, base=base, ph=ph: E.tensor_copy(Vt[:, base:base + n, :], PBH[ph][:, 0:n * 128].rearrange("p (n f) -> p n f", f=128)),
                     reads=[("pb", ph)], writes=[("Vt", i) for i, _ in batch])

        def vt_bank():
            if vt_state["n"] is None:
                vt_state["n"] = next_pb()
            return vt_state["n"]
        batch = []
        for bk in abks:
            vta[bk] = nvt
            off = a_blk_off(bk)
            slot = len(batch)
            ph = vt_bank()
            S.op("pe", lambda E, off=off, slot=slot, ph=ph: E.transpose(PBH[ph][:, slot * 128:(slot + 1) * 128], VaT[:, off:off + 128], identb[:, :]),
                 reads=["VaT", "identb"], writes=[("pb", ph)])
            batch.append((nvt, bk))
            nvt += 1
            if len(batch) == 4:
                flush_vt(batch, None)
                batch = []
        if batch:
            flush_vt(batch, None)
            batch = []
        pslot = {}
        ns = 0
        for j in range(4):
            for g in range(2):
                for blk in range(2):
                    bk = 4 * c + j - (1 - blk)
                    if bk < 0:
                        continue
                    b = next_pb()
                    off = a_blk_off(bk)
                    rv = QaP[:, g, :, 128 * j:128 * j + 128]
                    ov = PB[b][:, :]
                    ecol = E_A + (g * 2 + blk) * 512
                    S.op("pe", lambda E, off=off, rv=rv, ov=ov: E.matmul(ov, KaT[:, off:off + 128], rv, start=True, stop=False),
                         reads=["KaT", "QaP"], writes=[("pb", b)])
                    S.op("pe", lambda E, ov=ov, ecol=ecol: E.matmul(ov, identb[:, :], Ebf[:, ecol:ecol + 512], start=False, stop=True),
                         reads=["Ebf", "identb"], writes=[("pb", b)])
                    sl = ns
                    ns += 1
                    pslot[(j, g, blk)] = sl
                    S.op("act", lambda E, sl=sl, b=b: E.activation(hid[:, sl, :], PB[b][:, :], AF.Exp, scale=0.125),
                         reads=[("pb", b)], writes=[("hid", sl)])
        for hh in range(4):
            (NUMb, NK), (DENb, DK) = NUMS[hh % 2], DENS[hh % 2]
            first = [True, True]
            for j in range(4):
                for blk in range(2):
                    bk = 4 * c + j - (1 - blk)
                    if bk < 0:
                        continue
                    vt = vta[bk]
                    for hf in range(2):
                        rows = slice(64 * hf, 64 * hf + 64)
                        sl = pslot[(j, hf, blk)]
                        rhs = hid[:, sl, hh * 128:(hh + 1) * 128]
                        on = NUMb[rows, 128 * j:128 * j + 128]
                        od = DENb[rows, 128 * j:128 * j + 128]
                        st = first[hf]
                        first[hf] = False
                        S.op("pe", lambda E, on=on, vt=vt, hf=hf, rhs=rhs, st=st: E.matmul(on, Vt[:, vt, 64 * hf:64 * hf + 64], rhs, start=st, stop=False, skip_group_check=True),
                             reads=[("Vt", vt), ("hid", sl)], writes=[NK])
                        S.op("pe", lambda E, od=od, hf=hf, rhs=rhs, st=st: E.matmul(od, onesb[:, 0:64], rhs, start=st, stop=False, skip_group_check=True),
                             reads=[("hid", sl), "onesb"], writes=[DK])
            S.op("act", lambda E, hh=hh, DENb=DENb: E.activation(rden[:, :], DENb[:, :], AF.Ln, bias=esink[:, hh:hh + 1]), reads=[DK, "esink"], writes=["rden"])
            S.op("act", lambda E: E.activation(rden[:, :], rden[:, :], AF.Exp, scale=-1.0), reads=["rden"], writes=["rden"])
            S.op("dve", lambda E, hh=hh, NUMb=NUMb: E.tensor_tensor(xnT[:, hh, :], NUMb[:, :], rden[:, :], ALU.mult),
                 reads=[NK, "rden"], writes=[("xnT", hh)])
        cm = c % 4
        k1 = c // 4
        for pr in range(4):
            (NUMb, NK), (DENb, DK) = NUMS[pr % 2], DENS[pr % 2]
            vtb = {}
            nvt = 8
            batch = []
            def add_vt(key, ap):
                nonlocal nvt, batch
                slot = len(batch)
                vtb[key] = nvt
                ph = vt_bank()
                S.op("pe", lambda E, ap=ap, slot=slot, ph=ph: E.transpose(PBH[ph][:, slot * 128:(slot + 1) * 128], ap, identb[:, :]),
                     reads=["VbT", "identb"], writes=[("pb", ph)])
                batch.append((nvt, key))
                nvt += 1
                if len(batch) == 4:
                    flush_vt(batch, None)
                    batch = []
            for bk in abks:
                add_vt(("d1", bk), VbT[:, pr, 128 * bk:128 * bk + 128])
            for r4 in range(4):
                for cb_ in (c - 1, c):
                    if cb_ < 0:
                        continue
                    st_ = 512 * cb_ + r4
                    add_vt(("d4", r4, cb_), VbT[:, pr, st_:st_ + 509:4])
            for r in range(16):
                for kb in (k1 - 1, k1):
                    if kb < 0:
                        continue
                    st_ = 2048 * kb + r
                    add_vt(("d16", r, kb), VbT[:, pr, st_:st_ + 2033:16])
            if batch:
                flush_vt(batch, None)
                batch = []
            ps_ = {}
            ns = 0
            for j in range(4):
                b = next_pb()
                sl = ns
                ns += 1
                used = []
                for blk in range(2):
                    bk = 4 * c + j - (1 - blk)
                    if bk < 0:
                        continue
                    used.append(blk)
                    rv = QbD[:, pr, :, 128 * j:128 * j + 128]
                    ov = PB[b][:, blk * 256:(blk + 1) * 256]
                    ecb = E_1 + pr * 512 + blk * 256
                    S.op("pe", lambda E, bk=bk, rv=rv, ov=ov, pr=pr: E.matmul(ov, KbT[:, pr, 128 * bk:128 * bk + 128], rv, start=True, stop=False),
                         reads=["KbT", "QbD"], writes=[("pb", b)])
                    S.op("pe", lambda E, ov=ov, ecb=ecb: E.matmul(ov, identb[:, :], Ebf[:, ecb:ecb + 256], start=False, stop=True),
                         reads=["Ebf", "identb"], writes=[("pb", b)])
                    ps_[("d1", j, blk)] = (sl, blk * 256)
                c0 = used[0] * 256
                c1 = (used[-1] + 1) * 256
                ecol = E_1 + pr * 512
                S.op("act", lambda E, sl=sl, b=b, c0=c0, c1=c1: E.activation(hid[:, sl, c0:c1], PB[b][:, c0:c1], AF.Exp, scale=0.125),
                     reads=[("pb", b)], writes=[("hid", sl)])
            for r4 in range(4):
                b = next_pb()
                sl = ns
                ns += 1
                used = []
                for blk in range(2):
                    cb_ = c - (1 - blk)
                    if cb_ < 0:
                        continue
                    used.append(blk)
                    st_ = 512 * cb_ + r4
                    rv = QbD[:, pr, :, r4:512:4]
                    ov = PB[b][:, blk * 256:(blk + 1) * 256]
                    ecb = E_4 + pr * 512 + blk * 256
                    S.op("pe", lambda E, st_=st_, rv=rv, ov=ov, pr=pr: E.matmul(ov, KbT[:, pr, st_:st_ + 509:4], rv, start=True, stop=False),
                         reads=["KbT", "QbD"], writes=[("pb", b)])
                    S.op("pe", lambda E, ov=ov, ecb=ecb: E.matmul(ov, identb[:, :], Ebf[:, ecb:ecb + 256], start=False, stop=True),
                         reads=["Ebf", "identb"], writes=[("pb", b)])
                    ps_[("d4", r4, blk)] = (sl, blk * 256)
                c0 = used[0] * 256
                c1 = (used[-1] + 1) * 256
                ecol = E_4 + pr * 512
                S.op("act", lambda E, sl=sl, b=b, c0=c0, c1=c1: E.activation(hid[:, sl, c0:c1], PB[b][:, c0:c1], AF.Exp, scale=0.125),
                     reads=[("pb", b)], writes=[("hid", sl)])
            for blk in range(2):
                kb = k1 - (1 - blk)
                if kb < 0:
                    continue
                for rh in range(2):
                    b = next_pb()
                    sl = ns
                    ns += 1
                    for rr in range(8):
                        r = rh * 8 + rr
                        st_ = 2048 * kb + r
                        rv = QbD[:, pr, :, r:512:16]
                        ov = PB[b][:, rr * 64:(rr + 1) * 64]
                        ecol = E_16 + pr * 512 + (cm * 2 + blk) * 64
                        S.op("pe", lambda E, st_=st_, rv=rv, ov=ov, pr=pr: E.matmul(ov, KbT[:, pr, st_:st_ + 2033:16], rv, start=True, stop=False),
                             reads=["KbT", "QbD"], writes=[("pb", b)])
                        S.op("pe", lambda E, ov=ov, ecol=ecol: E.matmul(ov, identb[:, :], Ebf[:, ecol:ecol + 64], start=False, stop=True),
                             reads=["Ebf", "identb"], writes=[("pb", b)])
                        ps_[("d16", r, blk)] = (sl, rr * 64)
                    S.op("act", lambda E, sl=sl, b=b: E.activation(hid[:, sl, :], PB[b][:, :], AF.Exp, scale=0.125),
                         reads=[("pb", b)], writes=[("hid", sl)])
            first = [True, True]

            def pv(vt, sl, col, width, out_num, out_den, rhs_view):
                for eo in range(2):
                    rows = slice(64 * eo, 64 * eo + 64)
                    rhs = rhs_view(hid[:, sl, col + eo * width:col + (eo + 1) * width])
                    st = first[eo]
                    first[eo] = False
                    on = out_num(rows)
                    od = out_den(rows)
                    S.op("pe", lambda E, on=on, vt=vt, eo=eo, rhs=rhs, st=st: E.matmul(on, Vt[:, vt, 64 * eo:64 * eo + 64], rhs, start=st, stop=False, skip_group_check=True),
                         reads=[("Vt", vt), ("hid", sl)], writes=[NK])
                    S.op("pe", lambda E, od=od, rhs=rhs, st=st: E.matmul(od, onesb[:, 0:64], rhs, start=st, stop=False, skip_group_check=True),
                         reads=[("hid", sl), "onesb"], writes=[DK])
            for j in range(4):
                for blk in range(2):
                    bk = 4 * c + j - (1 - blk)
                    if bk < 0:
                        continue
                    sl, col = ps_[("d1", j, blk)]
                    pv(vtb[("d1", bk)], sl, col, 128,
                       lambda rows, j=j, NUMb=NUMb: NUMb[rows, 128 * j:128 * j + 128],
                       lambda rows, j=j, DENb=DENb: DENb[rows, 128 * j:128 * j + 128],
                       lambda ap: ap)
            for r4 in range(4):
                for blk in range(2):
                    cb_ = c - (1 - blk)
                    if cb_ < 0:
                        continue
                    sl, col = ps_[("d4", r4, blk)]
                    pv(vtb[("d4", r4, cb_)], sl, col, 128,
                       lambda rows, r4=r4, NUMb=NUMb: NUMb[rows, r4:512:4],
                       lambda rows, r4=r4, DENb=DENb: DENb[rows, r4:512:4],
                       lambda ap: ap)
            for r in range(16):
                for blk in range(2):
                    kb = k1 - (1 - blk)
                    if kb < 0:
                        continue
                    sl, col = ps_[("d16", r, blk)]
                    pv(vtb[("d16", r, kb)], sl, col, 32,
                       lambda rows, r=r, NUMb=NUMb: NUMb[rows, r:512:16],
                       lambda rows, r=r, DENb=DENb: DENb[rows, r:512:16],
                       lambda ap: ap)
            S.op("act", lambda E, DENb=DENb: E.activation(rden[:, :], DENb[:, :], AF.Ln), reads=[DK], writes=["rden"])
            S.op("act", lambda E: E.activation(rden[:, :], rden[:, :], AF.Exp, scale=-1.0), reads=["rden"], writes=["rden"])
            S.op("dve", lambda E, pr=pr, NUMb=NUMb: E.tensor_tensor(xnT[:, 4 + pr, :], NUMb[:, :], rden[:, :], ALU.mult),
                 reads=[NK, "rden"], writes=[("xnT", 4 + pr)])
        S.alias_barrier([("Vt", i) for i in range(64)], [("fT", i) for i in range(8)])

    def out_proj():
        pend = []
        for cb in range(4):
            s = use_block("out")
            wv = slot_view(s, 8, 256)
            for ti in range(2):
                m = 2 * cb + ti
                b = next_pb()
                for k in range(8):
                    S.op("pe", lambda E, k=k, wv=wv, ti=ti, b=b: E.matmul(PB[b][:, :], wv[:, k, ti * 128:(ti + 1) * 128], xnT[:, k, :], start=(k == 0), stop=(k == 7)),
                         reads=[("w", s), ("xnT", k)], writes=[("pb", b)])
                bias = vecs[:, NV_BOUT + m:NV_BOUT + m + 1]
                si = state["sq"]
                state["sq"] ^= 1
                S.op("dve", lambda E, m=m, b=b, bias=bias: E.tensor_scalar(fT[:, m, :], PB[b][:, :], bias, None, ALU.add),
                     reads=[("pb", b), "vecs"], writes=[("fT", m)])
                S.op("act", lambda E, si=si, m=m: E.activation(sq[:, si, :], fT[:, m, :], AF.Square),
                     reads=[("fT", m)], writes=[("sq", si)])
                for p_ in pend:
                    ssq_accum(*p_)
                pend = [(m, si)]
            issue_block()
        for p_ in pend:
            ssq_accum(*p_)
        finish_rstd(D * EPS)
        post_update(G_ATT_POST)

    def ple(c, after_gate=None, defer_stats=False):
        for tt in range(4):
            r0 = c * T + tt * 128
            S.op("pool", lambda E, tt=tt, r0=r0: E.dma_start(out=pin[:, tt, :], in_=p_d[r0:r0 + 128, :]),
                 writes=["rden"], dma=("p", tt))
        for k2 in range(2):
            bt = next_pb()
            for tt in range(4):
                S.op("pe", lambda E, k2=k2, tt=tt, bt=bt: E.transpose(PBH[bt][:, tt * 128:(tt + 1) * 128], pin[:, tt, k2 * 128:(k2 + 1) * 128], identb[:, :]),
                     reads=["rden", "identb"], writes=[("pb", bt)])
            S.op("act", lambda E, k2=k2, bt=bt: E.activation(pT[:, k2, :], PBH[bt][:, 0:512], AF.Identity), reads=[("pb", bt)], writes=["tmpA"])
        pre_norm(G_PLE_PRE)
        sp_ = None
        gate_slots = []
        pend = []
        for cb in range(4):
            s = use_block("gate")
            gate_slots.append(s)
            wv = slot_view(s, 8, 256)
            if cb == 0:
                sp_ = None
            for ti in range(2):
                m = 2 * cb + ti
                bgt = next_pb()
                for k in range(8):
                    S.op("pe", lambda E, k=k, wv=wv, ti=ti, b=bgt: E.matmul(PB[b][:, :], wv[:, k, ti * 128:(ti + 1) * 128], xnT[:, k, :], start=(k == 0), stop=(k == 7)),
                         reads=[("w", s), ("xnT", k)], writes=[("pb", bgt)])
                S.op("act", lambda E, m=m, b=bgt: E.activation(fT[:, m, :], PB[b][:, :], AF.Tanh, scale=0.5),
                     reads=[("pb", bgt)], writes=[("fT", m)])
            issue_block()
        if after_gate is not None:
            after_gate()
        s = use_block("proj")
        wv = slot_view(s, 2, 1024)
        for m in range(8):
            be = next_pb()
            for k2 in range(2):
                S.op("pe", lambda E, k2=k2, wv=wv, m=m, b=be: E.matmul(PB[b][:, :], wv[:, k2, m * 128:(m + 1) * 128], pT[:, k2, :], start=(k2 == 0), stop=(k2 == 1)),
                     reads=[("w", s), "tmpA"], writes=[("pb", be)])
            S.op("dve", lambda E, m=m, b=be: E.scalar_tensor_tensor(fT[:, m, :], fT[:, m, :], 1.0, PB[b][:, :], ALU.add, ALU.mult),
                 reads=[("fT", m), ("pb", be)], writes=[("fT", m)])
            if defer_stats:
                S.op("act", lambda E, m=m: E.activation(hid[:, 14 + m, :], fT[:, m, :], AF.Square), reads=[("fT", m)], writes=[("hid", 14 + m)])
                continue
            si = state["sq"]
            state["sq"] ^= 1
            S.op("act", lambda E, m=m, si=si: E.activation(sq[:, si, :], fT[:, m, :], AF.Square), reads=[("fT", m)], writes=[("sq", si)])
            for p_ in pend:
                ssq_accum(*p_)
            pend = [(m, si)]
        issue_block()
        for p_ in pend:
            ssq_accum(*p_)

    def ple_tail():
        finish_rstd(4.0 * D * EPS)
        post_update(G_PLE_POST)

    def ple_tail_stats():
        for m in range(8):
            for tt in range(4):
                S.op("pe", lambda E, m=m, tt=tt: E.matmul(DEN[:, tt:tt + 1], hid[:, 14 + m, tt * 128:(tt + 1) * 128], onesb[:, 0:1],
                                                         start=(m == 0 and tt == 0), stop=False, skip_group_check=True),
                     reads=[("hid", 14 + m), "onesb"], writes=["DEN"])
        finish_rstd(4.0 * D * EPS, bcast=False)

    def ple_tail_update():
        rstd_bcast()
        post_update(G_PLE_POST)

    full = debug_stage is None
    for c in range(nchunks):
        if full:
            if c == 0:
                for tt in range(4):
                    early_stats_tile(0, tt)
                early_transposes(G_FFN1_PRE)
                hooks = {1: (lambda: load_x(0))}
            else:
                hooks = {1: ple_tail_stats, 2: ple_tail_update, 5: (lambda c=c: store_out(c - 1)), 8: (lambda c=c: load_x(c))}
            ffn(0, G_FFN1_PRE, G_FFN1_POST, skip_prenorm=True, hooks=hooks)
        else:
            load_x(c)
            if debug_stage == "x":
                store_out(c)
                continue
            ffn(0, G_FFN1_PRE, G_FFN1_POST)
        if debug_stage != "ffn1":
            in_proj(c)
            attention(c)
            if debug_stage == "mix" and c == nchunks - 1:
                dbg = {}
                import os
                only = os.environ.get("DBG_DUMPS", "")
                def dump(name, ap, n, keys):
                    if only and name not in only.split(","):
                        return
                    d = nc.dram_tensor(name, [128, n], BF16, kind="ExternalOutput").ap()
                    S.op("sp", lambda E, d=d, ap=ap: E.dma_start(out=d[:, :], in_=ap), reads=keys, dma="dbg_" + name)
                    dbg_keys.extend(keys)
                dump("dbg_mix", xnT[:, :, :].rearrange("p k t -> p (k t)"), 4096, [("xnT", k) for k in range(8)])
                dump("dbg_qa", QaP[:, :, :, :].rearrange("p a b t -> p (a b t)"), 4096, ["QaP"])
                dump("dbg_qb", QbD[:, :, :, :].rearrange("p a b t -> p (a b t)"), 4096, ["QbD"])
                dump("dbg_ka", KaT[:, :], 1024, ["KaT"])
                dump("dbg_va", VaT[:, :], 1024, ["VaT"])
                for pr in range(4):
                    dump(f"dbg_kb{pr}", KbT[:, pr, 0:512], 512, ["KbT"])
                    dump(f"dbg_vb{pr}", VbT[:, pr, 0:512], 512, ["VbT"])
            out_proj()
            if debug_stage not in ("attn", "mix"):
                if full and c + 1 < nchunks:
                    h2 = {3 + 3 * tt: (lambda c=c, tt=tt: early_stats_tile(c + 1, tt)) for tt in range(4)}
                    ffn(1, G_FFN2_PRE, G_FFN2_POST, hooks=h2)
                    ple(c, after_gate=lambda: early_transposes(G_FFN1_PRE), defer_stats=True)
                    continue
                ffn(1, G_FFN2_PRE, G_FFN2_POST)
                if debug_stage != "ffn2":
                    ple(c)
                    ple_tail()
        store_out(c)
    S.op("sp", lambda E: None, reads=[("fT", i) for i in range(8)], writes=[("fT", i) for i in range(8)] + dbg_keys)

    S.finalize()
    sem_ids = list(Sched.ENGS) + S.dma_keys
    sems = {}
    for i, sid in enumerate(sem_ids):
        sems[sid] = es.enter_context(nc.semaphore(f"s{i}"))
    block = es.enter_context(nc.Block())
    S.emit(nc, block, sems)
    es.close()
    return nc


_CACHE = {}


def kernel(**inputs):
    shared = _prep_shared(inputs)
    x = np.asarray(inputs["x"], dtype=np.float32)
    p = np.asarray(inputs["p"], dtype=np.float32)
    in_maps = []
    for b in range(NB):
        m = dict(shared)
        m["x"] = np.ascontiguousarray(x[b])
        m["p"] = np.ascontiguousarray(p[0, b])
        in_maps.append(m)
    nc = build_program()
    res = run_bass_kernel_spmd(nc, in_maps, core_ids=list(range(NB)))
    out = np.stack([np.asarray(res.results[b]["out"], dtype=np.float32) for b in range(NB)], axis=0)
    return out
```
